# Optimizing a Trainium2 kernel written in Bass

```python
import math
import jax, jax.numpy as jnp
from jax import lax
import numpy as np

D_MODEL = 1024
BATCH = 8
SEQ = 2048
DEPTH = 1
DEC_BATCH = 128
DEC_SEQ = 8
PAST_LEN = 2048
PAGE_SIZE = 128

POOL_WIDTH = D_MODEL // 2
POOL_WINDOWS = (2, 4, 8, 16)
N_POOL_GROUPS = len(POOL_WINDOWS)
POOL_GROUP_DIM = POOL_WIDTH // N_POOL_GROUPS
POOL_STATE = max(POOL_WINDOWS) - 1
N_HEADS = 4
HEAD_DIM = 64
V_HEAD_DIM = 2 * HEAD_DIM
QK_WIDTH = N_HEADS * 2 * HEAD_DIM
ATTN_WIDTH = N_HEADS * V_HEAD_DIM
PROJ_WIDTH = POOL_WIDTH + 2 * QK_WIDTH + ATTN_WIDTH
D_FF = 2816
CONV_WIDTH = 3
N_BUCKETS = 32
MAX_DISTANCE = 128
QBLOCK = 128
N_MOD = 6
EPS = 1e-6

kernel_name = "hymba_pool_diffattn_convffn_step"


def rmsnorm(x, g):
    xf = x.astype(jnp.float32)
    y = xf * lax.rsqrt(jnp.mean(xf * xf, axis=-1, keepdims=True) + EPS)
    return (y * g.astype(jnp.float32)).astype(x.dtype)


def rel_bucket(dist):
    max_exact = N_BUCKETS // 2
    d = jnp.maximum(dist, 1).astype(jnp.float32)
    large = max_exact + (jnp.log(d / max_exact) / math.log(MAX_DISTANCE / max_exact)
                         * (N_BUCKETS - max_exact)).astype(jnp.int32)
    large = jnp.minimum(large, N_BUCKETS - 1)
    return jnp.where(dist < max_exact, dist, large)


def diff_attend(q, k, v, q_pos, k_pos, rel_bias, lam):
    s = jnp.einsum('bqhcd,bkhcd->bhcqk', q, k,
                   preferred_element_type=jnp.float32) * (HEAD_DIM ** -0.5)
    dist = q_pos[:, None] - k_pos[None, :]
    bias = rel_bias.astype(jnp.float32)[rel_bucket(jnp.maximum(dist, 0))]
    s = s + jnp.transpose(bias, (2, 0, 1))[None, :, None]
    s = jnp.where((dist >= 0)[None, None, None], s, -jnp.inf)
    p = jax.nn.softmax(s, axis=-1)
    a = p[:, :, 0] - lam * p[:, :, 1]
    return jnp.einsum('bhqk,bkhd->bqhd', a.astype(v.dtype), v)


def causal_attention(q, k_all, v_all, pos0, rel_bias, lam):
    B, T = q.shape[0], q.shape[1]
    k_pos = jnp.arange(k_all.shape[1], dtype=jnp.int32)
    q_pos = pos0 + jnp.arange(T, dtype=jnp.int32)
    if T % QBLOCK == 0 and T > QBLOCK:
        def block(i):
            qb = lax.dynamic_slice_in_dim(q, i * QBLOCK, QBLOCK, axis=1)
            pb = lax.dynamic_slice_in_dim(q_pos, i * QBLOCK, QBLOCK)
            return diff_attend(qb, k_all, v_all, pb, k_pos, rel_bias, lam)
        o = lax.map(block, jnp.arange(T // QBLOCK))
        return jnp.moveaxis(o, 0, 1).reshape(B, T, N_HEADS, V_HEAD_DIM)
    return diff_attend(q, k_all, v_all, q_pos, k_pos, rel_bias, lam)


def pool_mix(u, prefix, pos0, w_pool, pool_scale):
    B, T = u.shape[0], u.shape[1]
    ext = jnp.concatenate([prefix, u], axis=1)
    extf = ext.astype(jnp.float32)
    cs = jnp.concatenate([jnp.zeros((B, 1, POOL_WIDTH), jnp.float32),
                          jnp.cumsum(extf, axis=1)], axis=1)
    pos = pos0 + jnp.arange(T, dtype=jnp.int32)
    outs = []
    for g, w in enumerate(POOL_WINDOWS):
        sl = slice(g * POOL_GROUP_DIM, (g + 1) * POOL_GROUP_DIM)
        hi = cs[:, POOL_STATE + 1:, sl]
        lo = cs[:, POOL_STATE + 1 - w:POOL_STATE + 1 - w + T, sl]
        cnt = jnp.minimum(pos + 1, w).astype(jnp.float32)[None, :, None]
        outs.append((hi - lo) / cnt - extf[:, POOL_STATE:, sl])
    d = jnp.stack(outs, axis=2).astype(u.dtype)
    y = jnp.einsum('btgc,gcd->btgd', d, w_pool).reshape(B, T, POOL_WIDTH)
    return y * pool_scale, ext[:, -POOL_STATE:]


def causal_dwconv(u, prefix, w, b):
    T = u.shape[1]
    ext = jnp.concatenate([prefix, u], axis=1)
    y = b + w[0] * ext[:, 0:T]
    for j in range(1, CONV_WIDTH):
        y = y + w[j] * ext[:, j:j + T]
    return y, ext[:, -(CONV_WIDTH - 1):]


def layer(x, c, pos0, pool_prefix, conv_prefix, k_past, v_past, lam_init,
          w_ada, b_ada, g_pre_mix, g_post_mix, g_pre_ffn, g_post_ffn, w_in, w_out,
          w_pool, pool_scale, lam_q1, lam_k1, lam_q2, lam_k2, g_head, rel_bias,
          w_up, conv_w, conv_b, w_down):
    B, T = x.shape[0], x.shape[1]
    mod = jax.nn.silu(c) @ w_ada + b_ada
    sh1, sc1, gt1, sh2, sc2, gt2 = [m[:, None] for m in jnp.split(mod, N_MOD, axis=-1)]
    h = rmsnorm(x, g_pre_mix) * (1 + sc1) + sh1
    z = h @ w_in
    u_pool, q, k, v = jnp.split(z, [POOL_WIDTH, POOL_WIDTH + QK_WIDTH,
                                    POOL_WIDTH + 2 * QK_WIDTH], axis=-1)
    q = q.reshape(B, T, N_HEADS, 2, HEAD_DIM)
    k = k.reshape(B, T, N_HEADS, 2, HEAD_DIM)
    v = v.reshape(B, T, N_HEADS, V_HEAD_DIM)
    y_pool, pool_state = pool_mix(u_pool, pool_prefix, pos0, w_pool, pool_scale)
    lam = (jnp.exp(jnp.sum(lam_q1.astype(jnp.float32) * lam_k1.astype(jnp.float32)))
           - jnp.exp(jnp.sum(lam_q2.astype(jnp.float32) * lam_k2.astype(jnp.float32)))
           + lam_init)
    k_all = k if k_past is None else jnp.concatenate([k_past, k], axis=1)
    v_all = v if v_past is None else jnp.concatenate([v_past, v], axis=1)
    o = causal_attention(q, k_all, v_all, pos0, rel_bias, lam)
    o = rmsnorm(o, g_head) * (1.0 - lam_init)
    mix = jnp.concatenate([y_pool, o.reshape(B, T, ATTN_WIDTH)], axis=-1) @ w_out
    x = x + gt1 * rmsnorm(mix, g_post_mix)
    h2 = rmsnorm(x, g_pre_ffn) * (1 + sc2) + sh2
    up, conv_state = causal_dwconv(h2 @ w_up, conv_prefix, conv_w, conv_b)
    gate, val = jnp.split(up, 2, axis=-1)
    f = (jax.nn.gelu(gate, approximate=True) * val) @ w_down
    x = x + gt2 * rmsnorm(f, g_post_ffn)
    return x, k, v, pool_state, conv_state


def setup_inputs(seed: int = 0) -> dict:
    key = jax.random.key(seed)
    ks = jax.random.split(key, 32)
    f32 = jnp.float32
    n_pages = PAST_LEN // PAGE_SIZE
    n_phys = (DEC_BATCH * n_pages * 5) // 4
    nrm = lambda k, shape, s: jax.random.normal(k, shape, f32) * s
    page_table = jax.random.permutation(ks[6], n_phys)[:DEC_BATCH * n_pages]
    page_table = page_table.reshape(DEC_BATCH, n_pages).astype(jnp.int32)
    return {
        "x_prompt": nrm(ks[0], (BATCH, SEQ, D_MODEL), 1.0),
        "x_sample": nrm(ks[1], (DEC_BATCH, DEC_SEQ, D_MODEL), 1.0),
        "c_prompt": nrm(ks[2], (BATCH, D_MODEL), 1.0),
        "c_sample": nrm(ks[3], (DEC_BATCH, D_MODEL), 1.0),
        "cache_k": nrm(ks[4], (DEPTH, n_phys, PAGE_SIZE, N_HEADS, 2, HEAD_DIM), 1.0),
        "cache_v": nrm(ks[5], (DEPTH, n_phys, PAGE_SIZE, N_HEADS, V_HEAD_DIM), 1.0),
        "page_table": page_table,
        "state_pool": nrm(ks[7], (DEPTH, DEC_BATCH, POOL_STATE, POOL_WIDTH), 1.0),
        "state_conv": nrm(ks[8], (DEPTH, DEC_BATCH, CONV_WIDTH - 1, 2 * D_FF), 1.0),
        "w_ada": nrm(ks[9], (DEPTH, D_MODEL, N_MOD * D_MODEL), 0.5 * D_MODEL ** -0.5),
        "b_ada": nrm(ks[10], (DEPTH, N_MOD * D_MODEL), 0.02),
        "g_pre_mix": 1.0 + nrm(ks[11], (DEPTH, D_MODEL), 0.02),
        "g_post_mix": 1.0 + nrm(ks[12], (DEPTH, D_MODEL), 0.02),
        "g_pre_ffn": 1.0 + nrm(ks[13], (DEPTH, D_MODEL), 0.02),
        "g_post_ffn": 1.0 + nrm(ks[14], (DEPTH, D_MODEL), 0.02),
        "w_in": nrm(ks[15], (DEPTH, D_MODEL, PROJ_WIDTH), D_MODEL ** -0.5),
        "w_out": nrm(ks[16], (DEPTH, POOL_WIDTH + ATTN_WIDTH, D_MODEL), (POOL_WIDTH + ATTN_WIDTH) ** -0.5),
        "w_pool": nrm(ks[17], (DEPTH, N_POOL_GROUPS, POOL_GROUP_DIM, POOL_GROUP_DIM), POOL_GROUP_DIM ** -0.5),
        "pool_scale": 1.0 + nrm(ks[18], (DEPTH, POOL_WIDTH), 0.1),
        "lam_q1": nrm(ks[19], (DEPTH, HEAD_DIM), 0.1),
        "lam_k1": nrm(ks[20], (DEPTH, HEAD_DIM), 0.1),
        "lam_q2": nrm(ks[21], (DEPTH, HEAD_DIM), 0.1),
        "lam_k2": nrm(ks[22], (DEPTH, HEAD_DIM), 0.1),
        "g_head": 1.0 + nrm(ks[23], (DEPTH, V_HEAD_DIM), 0.02),
        "rel_bias": nrm(ks[24], (N_BUCKETS, N_HEADS), 0.5),
        "w_up": nrm(ks[25], (DEPTH, D_MODEL, 2 * D_FF), D_MODEL ** -0.5),
        "conv_w": nrm(ks[26], (DEPTH, CONV_WIDTH, 2 * D_FF), CONV_WIDTH ** -0.5),
        "conv_b": nrm(ks[27], (DEPTH, 2 * D_FF), 0.02),
        "w_down": nrm(ks[28], (DEPTH, D_FF, D_MODEL), D_FF ** -0.5),
    }


def reference(x_prompt, x_sample, c_prompt, c_sample, cache_k, cache_v, page_table,
              state_pool, state_conv, w_ada, b_ada, g_pre_mix, g_post_mix, g_pre_ffn,
              g_post_ffn, w_in, w_out, w_pool, pool_scale, lam_q1, lam_k1, lam_q2, lam_k2,
              g_head, rel_bias, w_up, conv_w, conv_b, w_down):
    n_dec = page_table.shape[0]
    past_len = page_table.shape[1] * cache_k.shape[2]
    xp, xs = x_prompt, x_sample
    kp_l, vp_l, pp_l, cp_l = [], [], [], []
    ks_l, vs_l, ps_l, cs_l = [], [], [], []
    for l in range(DEPTH):
        lam_init = 0.8 - 0.6 * math.exp(-0.3 * l)
        params = (w_ada[l], b_ada[l], g_pre_mix[l], g_post_mix[l], g_pre_ffn[l], g_post_ffn[l],
                  w_in[l], w_out[l], w_pool[l], pool_scale[l], lam_q1[l], lam_k1[l],
                  lam_q2[l], lam_k2[l], g_head[l], rel_bias, w_up[l], conv_w[l], conv_b[l],
                  w_down[l])
        pool0 = jnp.zeros((xp.shape[0], POOL_STATE, POOL_WIDTH), xp.dtype)
        conv0 = jnp.zeros((xp.shape[0], CONV_WIDTH - 1, 2 * D_FF), xp.dtype)
        xp, kp, vp, pp, cp = layer(xp, c_prompt, 0, pool0, conv0, None, None, lam_init, *params)
        k_past = cache_k[l][page_table].reshape(n_dec, past_len, N_HEADS, 2, HEAD_DIM)
        v_past = cache_v[l][page_table].reshape(n_dec, past_len, N_HEADS, V_HEAD_DIM)
        xs, ksn, vsn, psn, csn = layer(xs, c_sample, past_len, state_pool[l], state_conv[l],
                                       k_past, v_past, lam_init, *params)
        kp_l.append(kp); vp_l.append(vp); pp_l.append(pp); cp_l.append(cp)
        ks_l.append(ksn); vs_l.append(vsn); ps_l.append(psn); cs_l.append(csn)
    return (xp, xs,
            jnp.stack(kp_l), jnp.stack(vp_l), jnp.stack(pp_l), jnp.stack(cp_l),
            jnp.stack(ks_l), jnp.stack(vs_l), jnp.stack(ps_l), jnp.stack(cs_l))
```

```python
import math
import numpy as np
import concourse.bass as bass
import concourse.mybir as mybir
from concourse.bass_utils import run_bass_kernel_spmd

F32 = mybir.dt.float32
BF16 = mybir.dt.bfloat16
I32 = mybir.dt.int32
AF = mybir.ActivationFunctionType
ALU = mybir.AluOpType

NCORES = 8
D = 1024
KC = 8
T = 2048
W = 512
NT = T // W
NS = 16
LS = 8
WS = NS * LS
NPG = 16
DFF = 2816
NFC = 22
EPS = 1e-6
LAM_INIT = 0.2
NEG = -10000.0
N_PHYS = 2560
STAGE = 99
SUB = 99
MICRO = 99
BG_US = {0: 5.0, 1: 6.0, 2: 10.0, 3: 5.0}
FL = 1152

PO = {}
_o = 0
for _n, _c in [("g_pre_mix", 8), ("g_post_mix", 8), ("g_pre_ffn", 8), ("g_post_ffn", 8), ("b_ada", 48),
               ("pool_scale", 4), ("g_head", 1), ("cw0", 44), ("cw1", 44), ("cw2", 44), ("cb", 44), ("rc", 64)]:
    PO[_n] = _o
    _o += _c
NPAR = _o

ENG = ["pe", "act", "dve", "pool", "sp"]
NDS = 40


class Sched:
    def __init__(self):
        self.ops = {e: [] for e in ENG}
        self.cnt = {e: 0 for e in ENG}
        self.lw = {}
        self.rd = {}
        self.known = {e: {} for e in ENG}
        self.dma_i = {"sp": 0, "pool": 0, "act": 0}
        self.out_tokens = []

    PSUM_KEYS = ("mm0", "mm1", "ssp", "st0", "st1", "accO", "accL", "misc", "fb0", "fb1", "fb2")

    def add(self, eng, fn, r=(), w=(), dma=False, out=False):
        w = list(w) + [k for k in r if k in self.PSUM_KEYS]
        r = [k for k in r if k not in self.PSUM_KEYS]
        deps = set()
        for b in r:
            if b in self.lw:
                deps.add(self.lw[b])
        for b in w:
            if b in self.lw:
                deps.add(self.lw[b])
            for t in self.rd.get(b, ()):
                deps.add(t)
        if dma:
            i = self.dma_i[eng]
            self.dma_i[eng] += 1
            k = i % NDS
            v = 16 * (i // NDS + 1)
            tok = (("d", eng, k), v)
            if v > 16:
                deps.add((("d", eng, k), v - 16))
        else:
            self.cnt[eng] += 1
            tok = (("e", eng), self.cnt[eng])
        kn = self.known[eng]
        mx = {}
        for (s, v) in deps:
            if s == ("e", "pe") and eng == "pe":
                continue
            if kn.get(s, 0) >= v:
                continue
            mx[s] = max(mx.get(s, 0), v)
        for s, v in mx.items():
            kn[s] = v
        self.ops[eng].append((list(mx.items()), fn, tok))
        for b in w:
            self.lw[b] = tok
            self.rd[b] = []
        for b in r:
            self.rd.setdefault(b, []).append(tok)
        if out:
            self.out_tokens.append(tok)
        return tok


def build_program():
    nc = bass.Bass("TRN2", target_bir_lowering=False)
    S = Sched()

    def din(name, shape, dt=F32):
        return nc.dram_tensor(name, list(shape), dt, kind="ExternalInput").ap()

    def dout(name, shape, dt=F32):
        return nc.dram_tensor(name, list(shape), dt, kind="ExternalOutput").ap()

    xT_p = din("xT_p", [D, T]); xT_s = din("xT_s", [D, WS])
    cT_d = din("cT", [128, KC, 17]); params_d = din("params", [128, NPAR])
    w_ada_d = din("w_ada_l", [12, 128, KC, 512]); w_in_d = din("w_in_l", [4, 128, KC, 512])
    w_out_d = din("w_out_l", [2, 128, KC, 512]); w_pool_d = din("w_pool_l", [128, 4, 128])
    w_up_d = din("w_up_l", [11, 128, 2, KC, 256]); w_down_d = din("w_down_l", [8, 128, KC, 512])
    lamv_d = din("lamv", [1, 256]); rb_d = din("rel_bias", [32, 4]); oneh_d = din("onehot", [33, FL])
    jmat_d = din("jmat", [128, 128])
    kv_d = din("kvcache_l", [N_PHYS * 128, 1024])
    pt_d = din("pt", [1, NS * NPG], I32)
    spool_d = din("spool_l", [128, 4, NS, 15]); sconv_d = din("sconv_l", [128, 44, NS, 2])

    yT_p = dout("yT_p", [D, T]); yT_s = dout("yT_s", [D, WS])
    kT_po = dout("kT_p", [512, T]); v_po = dout("v_p", [T, 512])
    pool_po = dout("pool_p", [128, 4, 15]); conv_po = dout("conv_p", [128, 44, 2])
    kT_so = dout("kT_s", [512, WS]); v_so = dout("v_s", [WS, 512])
    pool_so = dout("pool_s", [128, 4, NS, 15]); conv_so = dout("conv_s", [128, 44, NS, 2])
    fscr = nc.dram_tensor("fscr", [4, FL], F32, kind="Internal").ap()
    wscr = nc.dram_tensor("wscr", [25, 128, KC * 512], BF16, kind="Internal").ap()
    vscr = nc.dram_tensor("vscr", [NS, 8, 512], BF16, kind="Internal").ap()

    from contextlib import ExitStack
    es = ExitStack()

    def sb(name, shape, dt=F32):
        return es.enter_context(nc.sbuf_tensor("sb_" + name, list(shape), dt))

    def ps(name, dt=F32, cols=512):
        return es.enter_context(nc.psum_tensor("ps_" + name, [128, cols], dt))

    params = sb("params", [128, NPAR]); cT = sb("cT", [128, KC, 17]); cTs = sb("cTs", [128, KC, 17], BF16)
    modT = sb("modT", [128, 48, 17])
    MA = {k: sb("M" + k, [128, KC, 17]) for k in ["A1", "G1", "A2", "G2"]}
    ones_b = sb("ones_b", [128, 128], BF16)
    lamt = sb("lamt", [128, 256]); lamw = sb("lamw", [128, 8]); neglam = sb("neglam", [128, 1])
    ghs = sb("ghs", [128, 1])
    rbx = sb("rbx", [33, 4]); jmat = sb("jmat", [128, 128])
    Mb = sb("Mb", [128, 4, 256])
    nb1 = sb("nb1", [128, 4, 2, 8]); nb2 = sb("nb2", [8, 4, 2, 8])
    wpool = sb("wpool", [128, 4, 128], BF16)
    NWB = 3
    wbuf = [sb("wbuf%d" % i, [128, KC, 512], BF16) for i in range(NWB)]
    epsc = sb("epsc", [128, 1])
    xT = sb("xT", [128, KC, W]); hT = sb("hT", [128, KC, W], BF16)
    sqb = [sb("sqb%d" % i, [128, W], BF16) for i in range(2)]
    rstd = sb("rstd", [128, W]); tmpA = [sb("tmpA%d" % i, [128, W]) for i in range(2)]
    qT = sb("qT", [128, 4, W], BF16)
    KVb = sb("KVb", [128, T // 128, 1024], BF16)
    kTs = sb("kTs", [128, 4, WS], BF16)
    stg = [sb("stg%d" % i, [128, 512]) for i in range(2)]
    uT = sb("uT", [128, 4, (16 + W)])
    dT = sb("dT", [128, 4, W], BF16); mixin = sb("mixin", [128, KC, W], BF16)
    mT = sb("mT", [128, KC, W])
    pT = [sb("pT%d" % i, [128, W], BF16) for i in range(3)]
    atmp = stg
    oT = mT[:, 0:4, :]
    U = [sb("U%d" % i, [128, 2 * (2 + W)]) for i in range(2)]
    yv = [sb("yv%d" % i, [128, 2 * W]) for i in range(2)]
    pw = [U[i][:, 0:16 + W] for i in range(2)]
    rl = yv[0][:, 0:W]; On = [yv[0][:, W:2 * W], yv[1][:, 0:W]]
    halo = sb("halo", [128, 44, NS, 2])
    gT = sb("gT", [128, NFC, W], BF16)
    gflat = gT[:].rearrange("p a b -> p (a b)")
    oneh = gflat[0:33, 0:2 * FL].bitcast(F32)
    fsb = gflat[0:4, 2 * FL:4 * FL].bitcast(F32)
    vnew = gT[0:8, 0:NS, :]
    ptb = sb("ptb", [128, NS * NPG], I32); iop = sb("iop", [128, 1], I32); iof = sb("iof", [128, 1])
    idx = sb("idx", [128, NS * NPG], I32)
    NKR = 16
    kpg = [KVb[:, i, 0:512] for i in range(NKR)]
    vpg = [KVb[:, i, 512:1024] for i in range(NKR)]
    pTs = sb("pTs", [128, NPG, 64], BF16); pnew = sb("pnew", [8, 64], BF16)
    qbd = sb("qbd", [128, 4, NS, 16], BF16); oTs = sb("oTs", [128, 4, WS]); mixs = sb("mixs", [128, 4, WS], BF16)
    vnb = sb("vnb", [8, 2, 512], BF16)
    bgt = sb("bgt", [128, 384])

    mm = [ps("mm0"), ps("mm1")]; ssp = ps("ssp"); stp = [ps("st0"), ps("st1")]
    accO = ps("accO"); accL = ps("accL"); misc = ps("misc")

    sp_dma = lambda fn, r=(), w=(), out=False: S.add("sp", fn, r, w, dma=True, out=out)
    pool_dma = lambda fn, r=(), w=(), out=False: S.add("pool", fn, r, w, dma=True, out=out)

    def P(name, n=1, c=0):
        o = PO[name] + c
        return params[:, o:o + n]

    sp_dma(lambda e: e.dma_start(out=params[:], in_=params_d), w=["params"])
    sp_dma(lambda e: e.dma_start(out=cT[:], in_=cT_d), w=["cT"])
    sp_dma(lambda e: e.dma_start(out=lamt[:], in_=lamv_d.partition_broadcast(128)), w=["lamt"])
    sp_dma(lambda e: e.dma_start(out=rbx[0:32, :], in_=rb_d), w=["rbx0"])
    sp_dma(lambda e: e.dma_start(out=oneh[:], in_=oneh_d), w=["oneh"])
    sp_dma(lambda e: e.dma_start(out=jmat[:], in_=jmat_d), w=["jmat"])
    sp_dma(lambda e: e.dma_start(out=ptb[:], in_=pt_d.partition_broadcast(128)), w=["ptb"])
    pool_dma(lambda e: e.dma_start(out=wpool[:], in_=w_pool_d), w=["wpool"])
    S.add("dve", lambda e: e.memset(ones_b[:], 1.0), w=["ones_b"])
    S.add("dve", lambda e: e.memset(qbd[:], 0.0), w=["qbd"])
    S.add("dve", lambda e: e.memset(epsc[:], EPS), w=["epsc"])
    S.add("dve", lambda e: e.memset(rbx[32:33, :], NEG), w=["rbx1"])
    S.add("dve", lambda e: e.memset(halo[:], 0.0), w=["halo%d" % c_ for c_ in range(44)])
    S.add("dve", lambda e: e.memset(uT[:], 0.0), w=["uT"])
    S.add("pool", lambda e: e.iota(iop[:], pattern=[[0, 1]], base=0, channel_multiplier=1), w=["iop"])
    S.add("pool", lambda e: e.tensor_copy(out=iof[:], in_=iop[:]), r=["iop"], w=["iof"])
    S.add("pool", lambda e: e.tensor_scalar(out=idx[:], in0=ptb[:], scalar1=128.0, scalar2=iof[:, 0:1],
                                            op0=ALU.mult, op1=ALU.add), r=["ptb", "iof"], w=["idx"])
    S.add("dve", lambda e: e.tensor_tensor(out=lamt[:, 0:64], in0=lamt[:, 0:64], in1=lamt[:, 64:128], op=ALU.mult),
          r=["lamt"], w=["lamt"])
    S.add("dve", lambda e: e.tensor_tensor(out=lamt[:, 128:192], in0=lamt[:, 128:192], in1=lamt[:, 192:256], op=ALU.mult),
          r=["lamt"], w=["lamt"])
    S.add("dve", lambda e: e.reduce_sum(out=lamw[:, 0:1], in_=lamt[:, 0:64], axis=mybir.AxisListType.X), r=["lamt"], w=["lamw0"])
    S.add("dve", lambda e: e.reduce_sum(out=lamw[:, 1:2], in_=lamt[:, 128:192], axis=mybir.AxisListType.X), r=["lamt"], w=["lamw1"])
    S.add("act", lambda e: e.activation(out=lamw[:, 2:4], in_=lamw[:, 0:2], func=AF.Exp), r=["lamw0", "lamw1"], w=["lamw2"])
    S.add("dve", lambda e: e.tensor_tensor(out=lamw[:, 4:5], in0=lamw[:, 3:4], in1=lamw[:, 2:3], op=ALU.subtract), r=["lamw2"], w=["lamw4"])
    S.add("dve", lambda e: e.tensor_scalar(out=neglam[:], in0=lamw[:, 4:5], scalar1=-LAM_INIT, scalar2=None, op0=ALU.add),
          r=["lamw4"], w=["neglam"])
    S.add("dve", lambda e: e.tensor_scalar(out=ghs[:], in0=P("g_head"), scalar1=(1.0 - LAM_INIT), scalar2=None, op0=ALU.mult),
          r=["params"], w=["ghs"])

    fbanks = [mm[0], mm[1], ssp]
    for i in range(3):
        S.add("pe", lambda e, i=i: e.matmul(fbanks[i][0:4, 0:384], lhsT=rbx[:, :], rhs=oneh[:, i * 384:(i + 1) * 384], start=True, stop=True),
              r=["rbx0", "rbx1", "oneh"], w=["fb%d" % i])
        S.add("act", lambda e, i=i: e.activation(out=fsb[:, i * 384:(i + 1) * 384], in_=fbanks[i][0:4, 0:384], func=AF.Copy),
              r=["fb%d" % i], w=["fsb%d" % i])
    sp_dma(lambda e: e.dma_start(out=fscr, in_=fsb[:]), r=["fsb0", "fsb1", "fsb2"], w=["fscr"])
    Mpv = mT[:, 0:2, :].rearrange("p a b -> p (a b)").rearrange("p (h j) -> p h j", h=4)
    for h in range(4):
        sp_dma(lambda e, h=h: e.dma_start(out=Mpv[:, h, :], in_=bass.AP(tensor=fscr.tensor, offset=h * FL + 384, ap=[[1, 128], [1, 256]])),
               r=["fscr"], w=["Mp%d" % h])
    for h in range(4):
        bk = mm[h % 2]
        S.add("pe", lambda e, h=h, bk=bk: e.matmul(bk[:, 0:256], lhsT=jmat[:, :], rhs=Mpv[:, h, :], start=True, stop=True),
              r=["jmat", "Mp%d" % h], w=["mm%d" % (h % 2)])
        S.add("act", lambda e, h=h, bk=bk: e.activation(out=Mb[:, h, :], in_=bk[:, 0:256], func=AF.Copy),
              r=["mm%d" % (h % 2)], w=["Mb"])
    for h in range(4):
        for c in range(2):
            S.add("dve", lambda e, h=h, c=c: e.tensor_copy(out=nb1[:, h, c, :], in_=Mb[:, h, 128:136]), r=["Mb"], w=["nb1"])
            S.add("dve", lambda e, h=h, c=c: e.tensor_copy(out=nb2[:, h, c, :], in_=Mb[0:8, h, 0:8]), r=["Mb"], w=["nb2"])


    S.add("act", lambda e: e.activation(out=cTs[:], in_=cT[:], func=AF.Silu), r=["cT"], w=["cTs"])
    def mod_blocks(b0, b1):
        for blk in range(b0, b1):
            wb = wbuf[blk % 2]
            pool_dma(lambda e, blk=blk, wb=wb: e.dma_start(out=wb[:], in_=w_ada_d[blk]), w=["wbuf%d" % (blk % 2)])
            bk = mm[blk % 2]
            def mm_ada(e, wb=wb, bk=bk):
                last = None
                for j in range(4):
                    for kc in range(KC):
                        last = e.matmul(bk[:, j * 17:(j + 1) * 17], lhsT=wb[:, kc, j * 128:(j + 1) * 128], rhs=cTs[:, kc, :],
                                        start=(kc == 0), stop=(kc == KC - 1))
                return last
            S.add("pe", mm_ada, r=["wbuf%d" % (blk % 2), "cTs"], w=["mm%d" % (blk % 2)])
            S.add("dve", lambda e, blk=blk, bk=bk: e.tensor_tensor(
                out=modT[:, blk * 4:(blk + 1) * 4, :], in0=bk[:, 0:68].rearrange("p (j s) -> p j s", s=17),
                in1=P("b_ada", 4, blk * 4).unsqueeze(2).to_broadcast([128, 4, 17]), op=ALU.add),
                r=["mm%d" % (blk % 2), "params"], w=["modT%d" % blk])
    def derive(name, mo, gname, plus1):
        gb = P(gname, 8).unsqueeze(2).to_broadcast([128, KC, 17])
        mk = ["modT%d" % (mo // 4), "modT%d" % (mo // 4 + 1)]
        if plus1:
            S.add("dve", lambda e: e.scalar_tensor_tensor(out=MA[name][:], in0=modT[:, mo:mo + 8, :], scalar=1.0, in1=gb,
                                                          op0=ALU.add, op1=ALU.mult), r=mk + ["params"], w=["M" + name])
        else:
            S.add("dve", lambda e: e.tensor_tensor(out=MA[name][:], in0=modT[:, mo:mo + 8, :], in1=gb, op=ALU.mult),
                  r=mk + ["params"], w=["M" + name])
    mod_blocks(0, 4)
    derive("A1", 8, "g_pre_mix", True)

    def mod_rest():
        mod_blocks(4, 12)
        derive("G1", 16, "g_post_mix", False)
        derive("A2", 32, "g_pre_ffn", True); derive("G2", 40, "g_post_ffn", False)

    nmm = [0]
    converted = set()
    bgq = []
    bgstate = {"site": 0}


    def build_bg_units():
        for s in range(NS):
            for q in range(4):
                bgq.append(("pages", s, q))
            bgq.append(("final", s, 0))

    bgsched = {"tile": -1, "slots": [], "k": 0, "pending_gather": []}

    def bg_gather(s, q, slot):
        for p in range(4):
            n = s * NPG + q * 4 + p
            blk = slot * 4 + p
            pool_dma(lambda e, n=n, blk=blk: e.indirect_dma_start(out=KVb[:, blk, :], out_offset=None, in_=kv_d,
                     in_offset=bass.IndirectOffsetOnAxis(ap=idx[:, n:n + 1], axis=0)), r=["idx"], w=["kvb%d" % blk])

    def bg_pages(s, q, slot):
        pbuf = (q % 4) * 4
        def sc(e):
            last = None
            for p in range(4):
                blk = slot * 4 + p
                for h in range(4):
                    o = p * 64 + h * 16
                    last = e.matmul(misc[:, o:o + 16], lhsT=KVb[:, blk, h * 128:(h + 1) * 128], rhs=qbd[:, h, s, :], start=True, stop=True,
                                    skip_group_check=True)
            return last
        bkeys = ["kvb%d" % (slot * 4 + p) for p in range(4)]
        S.add("pe", sc, r=bkeys + ["qbd"], w=["misc"])
        nfar = 3 if q == 3 else 4
        S.add("act", lambda e: e.activation(out=pTs[:, pbuf:pbuf + nfar, :], in_=misc[:, 0:nfar * 64].rearrange("p (g x) -> p g x", x=64),
                                            func=AF.Exp, scale=0.125), r=["misc"], w=["pTs"])
        if q == 3:
            S.add("dve", lambda e: e.scalar_tensor_tensor(out=bgt[:, 0:64], in0=misc[:, 192:256], scalar=0.125,
                                                          in1=nb1[:].rearrange("p h c j -> p (h c j)"), op0=ALU.mult, op1=ALU.add),
                  r=["misc", "nb1"], w=["bgt0"])
            S.add("act", lambda e: e.activation(out=pTs[:, pbuf + 3, :], in_=bgt[:, 0:64], func=AF.Exp), r=["bgt0"], w=["pTs"])
        def pv(e):
            last = None
            for p in range(4):
                blk = slot * 4 + p
                for h in range(4):
                    e.matmul(misc[:, 256 + h * 16:256 + (h + 1) * 16], lhsT=KVb[:, blk, 512 + h * 128:512 + (h + 1) * 128],
                             rhs=pTs[:, pbuf + p, h * 16:(h + 1) * 16], start=(p == 0 and h == 0), stop=False, skip_group_check=True)
                last = e.matmul(misc[:, 320:384], lhsT=ones_b[:, :], rhs=pTs[:, pbuf + p, :], start=False, stop=(p == 3), skip_group_check=True)
            return last
        S.add("pe", pv, r=bkeys + ["pTs", "ones_b"], w=["misc"])
        if q == 0:
            S.add("dve", lambda e: e.tensor_copy(out=bgt[:, 256:384], in_=misc[:, 256:384]), r=["misc"], w=["bgacc"])
        else:
            S.add("dve", lambda e: e.tensor_tensor(out=bgt[:, 256:384], in0=misc[:, 256:384], in1=bgt[:, 256:384], op=ALU.add),
                  r=["misc", "bgacc"], w=["bgacc"])

    def bg_final(s):
        sp_dma(lambda e: e.dma_start(out=vnb[:, s % 2, :], in_=vscr[s]), r=["vscr%d" % s], w=["vnb%d" % (s % 2)])
        def scn(e):
            last = None
            for h in range(4):
                last = e.matmul(misc[0:8, 384 + h * 16:384 + (h + 1) * 16], lhsT=kTs[:, h, s * 8:(s + 1) * 8], rhs=qbd[:, h, s, :],
                                start=True, stop=True, skip_group_check=True)
            return last
        S.add("pe", scn, r=["kTs", "qbd"], w=["misc"])
        S.add("dve", lambda e: e.scalar_tensor_tensor(out=bgt[0:8, 192:256], in0=misc[0:8, 384:448], scalar=0.125,
                                                      in1=nb2[:].rearrange("p h c j -> p (h c j)"), op0=ALU.mult, op1=ALU.add),
              r=["misc", "nb2"], w=["bgn"])
        S.add("act", lambda e: e.activation(out=pnew[:, :], in_=bgt[0:8, 192:256], func=AF.Exp), r=["bgn"], w=["pnew"])
        def pvn(e):
            for h in range(4):
                e.matmul(misc[:, 256 + h * 16:256 + (h + 1) * 16], lhsT=vnb[:, s % 2, h * 128:(h + 1) * 128], rhs=pnew[:, h * 16:(h + 1) * 16],
                         start=(h == 0), stop=False, skip_group_check=True)
            return e.matmul(misc[:, 320:384], lhsT=ones_b[0:8, :], rhs=pnew[:, :], start=False, stop=True, skip_group_check=True)
        S.add("pe", pvn, r=["pnew", "vnb%d" % (s % 2), "ones_b"], w=["misc"])
        S.add("dve", lambda e: e.tensor_tensor(out=bgt[:, 256:384], in0=misc[:, 256:384], in1=bgt[:, 256:384], op=ALU.add),
              r=["misc", "bgacc"], w=["bgacc"])
        S.add("dve", lambda e: e.reciprocal(out=bgt[:, 64:128], in_=bgt[:, 320:384]), r=["bgacc"], w=["bgr"])
        S.add("dve", lambda e: e.tensor_tensor(out=bgt[:, 128:192], in0=bgt[:, 256:320], in1=bgt[:, 64:128], op=ALU.mult), r=["bgacc", "bgr"], w=["bgp"])
        a4 = bgt[:, 128:192].rearrange("p (h c j) -> p h c j", h=4, c=2)
        S.add("dve", lambda e: e.scalar_tensor_tensor(out=oTs[:, :, s * 8:(s + 1) * 8], in0=a4[:, :, 1, :], scalar=neglam[:, 0:1],
                                                      in1=a4[:, :, 0, :], op0=ALU.mult, op1=ALU.add),
              r=["bgp", "neglam"], w=["oTs"])

    def bg_begin(tile):
        slots = {0: [1, 2, 3], 1: [2, 3], 2: [3], 3: [0, 1, 2, 3], 99: [0, 1, 2, 3]}.get(tile, [])
        bgsched["tile"] = tile
        bgsched["slots"] = slots
        bgsched["k"] = 0
        bgsched["inflight"] = []
        pages = [u for u in bgq if u[0] == "pages"]
        for i in range(min(len(slots), len(pages))):
            u = pages[i]
            slot = slots[bgsched["k"] % len(slots)]
            bgsched["k"] += 1
            bg_gather(u[1], u[2], slot)
            bgsched["inflight"].append((u, slot))

    def bg_emit_one():
        if not bgq or not bgsched["slots"]:
            return False
        u = bgq[0]
        if u[0] == "final":
            bgq.pop(0)
            bg_final(u[1])
            return True
        if not bgsched["inflight"] or bgsched["inflight"][0][0] != u:
            return False
        bgq.pop(0)
        _, slot = bgsched["inflight"].pop(0)
        bg_pages(u[1], u[2], slot)
        gathered = set(x[0] for x in bgsched["inflight"])
        for v in bgq:
            if v[0] == "pages" and v not in gathered:
                bg_gather(v[1], v[2], slot)
                bgsched["inflight"].append((v, slot))
                break
        return True

    def bg_site(tile, cost=1.7):
        if tile != bgsched["tile"] or not bgq:
            return
        bgstate["site"] += cost
        per = BG_US.get(tile, 20.0)
        while bgstate["site"] >= per:
            bgstate["site"] -= per
            if not bg_emit_one():
                break

    def bg_flush():
        bg_begin(99)
        while bgq:
            if not bg_emit_one():
                raise RuntimeError("bg flush stuck")

    def run_tile(samp, t, part="AB"):
        nseg = NS if samp else 1
        L = LS if samp else W
        Wt = nseg * L
        c0 = 0 if samp else 16
        tag = "s" if samp else "p%d" % t

        def seg(ap2d):
            return ap2d.rearrange("p (s l) -> p s l", l=L)

        def mbc(tile3, kc):
            return tile3[:, kc, c0:c0 + nseg].unsqueeze(2).to_broadcast([128, nseg, L])

        if (not samp) and t < 3:
            bg_begin(t)
        if samp and part == "B":
            bg_flush()
            sp_dma(lambda e: e.dma_start(out=halo[:].rearrange("p a s l -> p (a s l)"), in_=sconv_d.rearrange("p a s l -> p (a s l)")), r=["halo%d" % c_ for c_ in range(44)], w=["halo%d" % c_ for c_ in range(44)])
            S.add("dve", lambda e: e.tensor_copy(out=mixin[:, 0:4, 0:WS], in_=mixs[:]), r=["mixs"], w=["mixin%d" % g for g in range(4)])
        elif (not samp) and t == 0:
            S.add("dve", lambda e: e.memset(uT[:, :, 0:16], 0.0), r=["uT"], w=["uT"])
        src = xT_s if samp else xT_p[:, t * W:(t + 1) * W]
        sp_dma(lambda e: e.dma_start(out=xT[:, :, 0:Wt], in_=src.rearrange("(k p) w -> p k w", p=128)), w=["xT%d" % k for k in range(KC)])

        def sumsq_rstd(srcfn, keys, nchunks, dim):
            if MICRO < 1:
                return
            for k in range(nchunks):
                S.add("act", lambda e, k=k: e.activation(out=sqb[k % 2][:, 0:Wt], in_=srcfn(k), func=AF.Square),
                      r=[keys[k]], w=["sqb%d" % (k % 2)])
                S.add("pe", lambda e, k=k: e.matmul(ssp[:, 0:Wt], lhsT=ones_b[:, :], rhs=sqb[k % 2][:, 0:Wt], start=(k == 0), stop=(k == nchunks - 1)),
                      r=["sqb%d" % (k % 2), "ones_b"], w=["ssp"])
            if MICRO < 2:
                return
            S.add("act", lambda e: e.activation(out=rstd[:, 0:Wt], in_=ssp[:, 0:Wt], func=AF.Sqrt, scale=1.0 / dim, bias=epsc[:, 0:1]),
                  r=["ssp", "epsc"], w=["rstd"])
            if MICRO < 3:
                return
            S.add("dve", lambda e: e.reciprocal(out=rstd[:, 0:Wt], in_=rstd[:, 0:Wt]), r=["rstd"], w=["rstd"])

        def pre_norm(Aname, bo):
            sumsq_rstd(lambda k: xT[:, k, 0:Wt], ["xT%d" % k for k in range(KC)], KC, D)
            if MICRO < 4:
                return
            for k in range(KC):
                tm = tmpA[k % 2]
                if not samp:
                    S.add("dve", lambda e, k=k, tm=tm: e.scalar_tensor_tensor(out=tm[:, 0:Wt], in0=xT[:, k, 0:Wt], scalar=MA[Aname][:, k, 16:17],
                                                                             in1=rstd[:, 0:Wt], op0=ALU.mult, op1=ALU.mult),
                          r=["xT%d" % k, "rstd", "M" + Aname], w=["tmpA%d" % (k % 2)])
                    S.add("act", lambda e, k=k, tm=tm: e.activation(out=hT[:, k, 0:Wt], in_=tm[:, 0:Wt], func=AF.Identity, bias=modT[:, bo + k, 16:17], scale=1.0),
                          r=["tmpA%d" % (k % 2), "modT%d" % ((bo + k) // 4)], w=["hT%d" % k])
                    continue
                S.add("dve", lambda e, k=k, tm=tm: e.tensor_tensor(out=tm[:, 0:Wt], in0=xT[:, k, 0:Wt], in1=rstd[:, 0:Wt], op=ALU.mult),
                      r=["xT%d" % k, "rstd"], w=["tmpA%d" % (k % 2)])
                if MICRO < 5:
                    continue
                S.add("dve", lambda e, k=k, tm=tm: e.tensor_tensor(out=seg(tm[:, 0:Wt]), in0=seg(tm[:, 0:Wt]), in1=mbc(MA[Aname], k), op=ALU.mult),
                      r=["tmpA%d" % (k % 2), "M" + Aname], w=["tmpA%d" % (k % 2)])
                S.add("dve", lambda e, k=k, tm=tm: e.tensor_tensor(out=seg(hT[:, k, 0:Wt]), in0=seg(tm[:, 0:Wt]),
                                                                  in1=modT[:, bo + k, c0:c0 + nseg].unsqueeze(2).to_broadcast([128, nseg, L]), op=ALU.add),
                      r=["tmpA%d" % (k % 2), "modT%d" % ((bo + k) // 4)], w=["hT%d" % k])

        def rstd_from_ssp(dim):
            S.add("act", lambda e: e.activation(out=rstd[:, 0:Wt], in_=ssp[:, 0:Wt], func=AF.Sqrt, scale=1.0 / dim, bias=epsc[:, 0:1]),
                  r=["ssp", "epsc"], w=["rstd"])
            S.add("dve", lambda e: e.reciprocal(out=rstd[:, 0:Wt], in_=rstd[:, 0:Wt]), r=["rstd"], w=["rstd"])

        pend = []

        def evac_mix(b, oc, Gname):
            if samp:
                S.add("act", lambda e: e.activation(out=mT[:, oc, 0:Wt], in_=mm[b][:, 0:Wt], func=AF.Copy), r=["mm%d" % b], w=["mT%d" % oc])
                return
            S.add("act", lambda e: e.activation(out=mT[:, oc, 0:Wt], in_=mm[b][:, 0:Wt], func=AF.Copy, scale=MA[Gname][:, oc, 16:17]),
                  r=["mm%d" % b, "M" + Gname], w=["mT%d" % oc])
            S.add("act", lambda e: e.activation(out=sqb[oc % 2][:, 0:Wt], in_=mm[b][:, 0:Wt], func=AF.Square), r=["mm%d" % b], w=["sqb%d" % (oc % 2)])
            pend.append(oc)

        def flush_sq(keep=0):
            while len(pend) > keep:
                oc = pend.pop(0)
                S.add("pe", lambda e, oc=oc: e.matmul(ssp[:, 0:Wt], lhsT=ones_b[:, :], rhs=sqb[oc % 2][:, 0:Wt], start=(oc == 0), stop=(oc == KC - 1)),
                      r=["sqb%d" % (oc % 2), "ones_b"], w=["ssp"])

        def post_norm_res(Gname):
            if not samp:
                flush_sq(0)
                rstd_from_ssp(D)
                for k in range(KC):
                    tm = tmpA[k % 2]
                    S.add("dve", lambda e, k=k, tm=tm: e.tensor_tensor(out=tm[:, 0:Wt], in0=mT[:, k, 0:Wt], in1=rstd[:, 0:Wt], op=ALU.mult),
                          r=["mT%d" % k, "rstd"], w=["tmpA%d" % (k % 2)])
                    S.add("dve", lambda e, k=k, tm=tm: e.tensor_tensor(out=xT[:, k, 0:Wt], in0=xT[:, k, 0:Wt], in1=tm[:, 0:Wt], op=ALU.add),
                          r=["tmpA%d" % (k % 2), "xT%d" % k], w=["xT%d" % k])
                return
            sumsq_rstd(lambda k: mT[:, k, 0:Wt], ["mT%d" % k for k in range(KC)], KC, D)
            for k in range(KC):
                tm = tmpA[k % 2]
                S.add("dve", lambda e, k=k, tm=tm: e.tensor_tensor(out=tm[:, 0:Wt], in0=mT[:, k, 0:Wt], in1=rstd[:, 0:Wt], op=ALU.mult),
                      r=["mT%d" % k, "rstd"], w=["tmpA%d" % (k % 2)])
                S.add("dve", lambda e, k=k, tm=tm: e.tensor_tensor(out=seg(tm[:, 0:Wt]), in0=seg(tm[:, 0:Wt]), in1=mbc(MA[Gname], k), op=ALU.mult),
                      r=["tmpA%d" % (k % 2), "M" + Gname], w=["tmpA%d" % (k % 2)])
                S.add("dve", lambda e, k=k, tm=tm: e.tensor_tensor(out=xT[:, k, 0:Wt], in0=xT[:, k, 0:Wt], in1=tm[:, 0:Wt], op=ALU.add),
                      r=["tmpA%d" % (k % 2), "xT%d" % k], w=["xT%d" % k])

        hkeys = ["hT%d" % k for k in range(KC)]
        front = (part != "B")
        if front:
            pre_norm("A1", 0)
        def load_w(dsrc, bid):
            i = nmm[0] % NWB
            nmm[0] += 1
            wflat = wbuf[i][:].rearrange("p k c -> p (k c)")
            if bid not in converted:
                converted.add(bid)
                pool_dma(lambda e, i=i: e.dma_start(out=wbuf[i][:], in_=dsrc), w=["wbuf%d" % i])
                sp_dma(lambda e: e.dma_start(out=wscr[bid], in_=wflat), r=["wbuf%d" % i], w=["wscr%d" % bid])
            else:
                sp_dma(lambda e: e.dma_start(out=wflat, in_=wscr[bid]), r=["wscr%d" % bid], w=["wbuf%d" % i])
            return i
        mmi = [0]
        def fm_chunk(wtile, wkey, col, evac):
            b = mmi[0] % 2
            mmi[0] += 1
            def f(e):
                last = None
                for kc in range(KC):
                    last = e.matmul(mm[b][:, 0:Wt], lhsT=wtile[:, kc, col:col + 128], rhs=hT[:, kc, 0:Wt], start=(kc == 0), stop=(kc == KC - 1))
                return last
            S.add("pe", f, r=[wkey] + hkeys, w=["mm%d" % b])
            evac(mm[b], "mm%d" % b)
            if not samp:
                bg_site(t)

        if front:
            for blk in [0, 1, 2]:
                wi = load_w(w_in_d[blk], blk)
                for j in range(4):
                    if blk == 0:
                        def ev(bk, bkey, j=j):
                            if samp:
                                S.add("act", lambda e: e.activation(out=uT[:, j, 0:NS * 24].rearrange("p (s l) -> p s l", l=24)[:, :, 16:24],
                                                                    in_=seg(bk[:, 0:Wt]), func=AF.Copy), r=[bkey], w=["uT"])
                            else:
                                S.add("act", lambda e: e.activation(out=uT[:, j, 16:16 + W], in_=bk[:, 0:W], func=AF.Copy), r=[bkey], w=["uT"])
                    elif blk == 1:
                        def ev(bk, bkey, j=j):
                            if samp:
                                for c_ in range(2):
                                    S.add("act", lambda e, c_=c_: e.activation(out=qbd[c_ * 64:(c_ + 1) * 64, j, :, c_ * 8:(c_ + 1) * 8],
                                                                              in_=bk[c_ * 64:(c_ + 1) * 64, 0:Wt].rearrange("p (s l) -> p s l", l=8), func=AF.Copy),
                                          r=[bkey, "qbd"], w=["qbd"])
                            else:
                                S.add("act", lambda e: e.activation(out=qT[:, j, 0:Wt], in_=bk[:, 0:Wt], func=AF.Copy), r=[bkey], w=["qT"])
                    else:
                        def ev(bk, bkey, j=j):
                            si = j % 2
                            S.add("act", lambda e: e.activation(out=stg[si][:, 0:Wt], in_=bk[:, 0:Wt], func=AF.Copy), r=[bkey], w=["stg%d" % si])
                            if samp:
                                S.add("dve", lambda e: e.tensor_copy(out=kTs[:, j, :], in_=bk[:, 0:Wt]), r=[bkey], w=["kTs"])
                                sp_dma(lambda e: e.dma_start(out=kT_so[j * 128:(j + 1) * 128, :], in_=stg[si][:, 0:Wt]), r=["stg%d" % si], out=True)
                            else:
                                S.add("dve", lambda e: e.tensor_copy(out=KVb[:, 4 * t:4 * t + 4, j * 128:(j + 1) * 128], in_=bk[:, 0:W].rearrange("p (b x) -> p b x", x=128)), r=[bkey], w=["kvb%d" % (4 * t + q_) for q_ in range(4)])
                                sp_dma(lambda e: e.dma_start(out=kT_po[j * 128:(j + 1) * 128, t * W:(t + 1) * W], in_=stg[si][:, 0:W]), r=["stg%d" % si], out=True)
                    fm_chunk(wbuf[wi], "wbuf%d" % wi, j * 128, ev)
            wi = load_w(w_in_d[3], 3)
            if samp:
                for s in range(NS):
                    b = mmi[0] % 2
                    mmi[0] += 1
                    def f(e, s=s, b=b, wi=wi):
                        last = None
                        for kc in range(KC):
                            last = e.matmul(mm[b][0:8, :], lhsT=hT[:, kc, s * 8:(s + 1) * 8], rhs=wbuf[wi][:, kc, :], start=(kc == 0), stop=(kc == KC - 1))
                        return last
                    S.add("pe", f, r=["wbuf%d" % wi] + hkeys, w=["mm%d" % b])
                    si = s % 2
                    S.add("act", lambda e, b=b, si=si: e.activation(out=stg[si][0:8, :], in_=mm[b][0:8, :], func=AF.Copy), r=["mm%d" % b], w=["stg%d" % si])
                    S.add("dve", lambda e, b=b, s=s: e.tensor_copy(out=vnb[:, s % 2, :], in_=mm[b][0:8, :]), r=["mm%d" % b], w=["vnb%d" % (s % 2)])
                    sp_dma(lambda e, s=s: e.dma_start(out=vscr[s], in_=vnb[:, s % 2, :]), r=["vnb%d" % (s % 2)], w=["vscr%d" % s])
                    sp_dma(lambda e, s=s, si=si: e.dma_start(out=v_so[s * 8:(s + 1) * 8, :], in_=stg[si][0:8, :]), r=["stg%d" % si], out=True)
            else:
                for bb in range(4):
                    b = mmi[0] % 2
                    mmi[0] += 1
                    def f(e, bb=bb, b=b, wi=wi):
                        last = None
                        for kc in range(KC):
                            last = e.matmul(mm[b][:, :], lhsT=hT[:, kc, bb * 128:(bb + 1) * 128], rhs=wbuf[wi][:, kc, :], start=(kc == 0), stop=(kc == KC - 1))
                        return last
                    S.add("pe", f, r=["wbuf%d" % wi] + hkeys, w=["mm%d" % b])
                    si = bb % 2
                    S.add("act", lambda e, b=b, si=si: e.activation(out=stg[si][:, :], in_=mm[b][:, :], func=AF.Copy), r=["mm%d" % b], w=["stg%d" % si])
                    S.add("dve", lambda e, b=b, bb=bb: e.tensor_copy(out=KVb[:, t * 4 + bb, 512:1024], in_=mm[b][:, :]), r=["mm%d" % b], w=["kvb%d" % (t * 4 + bb)])
                    sp_dma(lambda e, bb=bb, si=si: e.dma_start(out=v_po[t * W + bb * 128:t * W + (bb + 1) * 128, :], in_=stg[si][:, :]), r=["stg%d" % si], out=True)
                    bg_site(t)

            SL = 16 + L
            def useg(g):
                return uT[:, g, 0:nseg * SL].rearrange("p (s l) -> p s l", l=SL)
            def pseg(i):
                return pw[i][:, 0:nseg * SL].rearrange("p (s l) -> p s l", l=SL)
            if samp:
                for g in range(4):
                    sp_dma(lambda e, g=g: e.dma_start(out=uT[:, g, 0:NS * 24].rearrange("p (s l) -> p s l", l=24)[:, :, 1:16], in_=spool_d[:, g, :, :]), r=["uT"], w=["uT"])
            for g in range(4):
                cur = useg(g)
                sh = 1
                for step in range(g + 1):
                    dst = pseg(step % 2)
                    lo = 1 + sh if step == 0 else 1 + 2 * sh - 1
                    lo = {0: 2, 1: 4, 2: 8, 3: 16}[step]
                    S.add("pool", lambda e, cur=cur, dst=dst, lo=lo, sh=sh: e.tensor_tensor(out=dst[:, :, lo:SL], in0=cur[:, :, lo:SL], in1=cur[:, :, lo - sh:SL - sh], op=ALU.add),
                          r=["uT", "pw0", "pw1"], w=["pw%d" % (step % 2)])
                    cur = dst
                    sh *= 2
                wwin = 2 ** (g + 1)
                S.add("dve", lambda e, g=g, cur=cur, wwin=wwin: e.scalar_tensor_tensor(
                    out=seg(dT[:, g, 0:Wt]), in0=cur[:, :, 16:SL], scalar=1.0 / wwin, in1=useg(g)[:, :, 16:SL], op0=ALU.mult, op1=ALU.subtract),
                    r=["pw0", "pw1", "uT"], w=["dT%d" % g])
                if (not samp) and t == 0:
                    S.add("dve", lambda e, g=g, cur=cur: e.tensor_tensor(out=tmpA[0][:, 0:16], in0=cur[:, 0, 16:32], in1=P("rc", 16, g * 16), op=ALU.mult),
                          r=["pw0", "pw1", "params"], w=["tmpA0"])
                    S.add("dve", lambda e, g=g: e.tensor_tensor(out=dT[:, g, 0:16], in0=tmpA[0][:, 0:16], in1=uT[:, g, 16:32], op=ALU.subtract),
                          r=["tmpA0", "uT", "dT%d" % g], w=["dT%d" % g])
                b = mmi[0] % 2
                mmi[0] += 1
                S.add("pe", lambda e, g=g, b=b: e.matmul(mm[b][:, 0:Wt], lhsT=wpool[:, g, :], rhs=dT[:, g, 0:Wt], start=True, stop=True),
                      r=["wpool", "dT%d" % g], w=["mm%d" % b])
                if samp:
                    S.add("act", lambda e, g=g, b=b: e.activation(out=mixs[:, g, :], in_=mm[b][:, 0:Wt], func=AF.Copy, scale=P("pool_scale", 1, g)),
                          r=["mm%d" % b, "params"], w=["mixs"])
                else:
                    S.add("act", lambda e, g=g, b=b: e.activation(out=mixin[:, g, 0:Wt], in_=mm[b][:, 0:Wt], func=AF.Copy, scale=P("pool_scale", 1, g)),
                          r=["mm%d" % b, "params"], w=["mixin%d" % g])
            if samp:
                for g in range(4):
                    sp_dma(lambda e, g=g: e.dma_start(out=pool_so[:, g, :, :], in_=uT[:, g, 0:NS * 24].rearrange("p (s l) -> p s l", l=24)[:, :, 9:24]), r=["uT"], out=True)
            else:
                if t == NT - 1:
                    sp_dma(lambda e: e.dma_start(out=pool_po, in_=uT[:, :, 16 + W - 15:16 + W]), r=["uT"], out=True)
                else:
                    S.add("pool", lambda e: e.tensor_copy(out=uT[:, :, 0:16], in_=uT[:, :, W:W + 16]), r=["uT"], w=["uT"])

        if samp and part == "A":
            build_bg_units()
            return
        def finish_head(h, c, ncols):
            S.add("dve", lambda e: e.reciprocal(out=rl[:, 0:ncols], in_=accL[:, 0:ncols]), r=["accL"], w=["rl"])
            S.add("dve", lambda e: e.tensor_tensor(out=On[c][:, 0:ncols], in0=accO[:, 0:ncols], in1=rl[:, 0:ncols], op=ALU.mult),
                  r=["accO", "rl"], w=["On%d" % c])

        def head_norm(h):
            S.add("dve", lambda e: e.scalar_tensor_tensor(out=oT[:, h, 0:Wt], in0=On[1][:, 0:Wt], scalar=neglam[:, 0:1], in1=On[0][:, 0:Wt],
                                                          op0=ALU.mult, op1=ALU.add), r=["On0", "On1", "neglam"], w=["mT%d" % h])

        if not samp:
            nkb = 4 * t + 4
            blocks = [(h, c, kb) for h in range(4) for kb in range(nkb) for c in range(2)]
            accOs = [(accO, "accO"), (mm[0], "mm0")]
            accLs = [(accL, "accL"), (mm[1], "mm1")]

            def emit_S(i):
                h, c, kb = blocks[i]
                m = kb - 4 * t
                col0 = 128 * m if m >= 1 else 0
                near = (kb >= 4 * t - 1)
                sb_ = i % 2
                pb = i % 3
                S.add("pe", lambda e: e.matmul(stp[sb_][:, col0:W], lhsT=KVb[c * 64:(c + 1) * 64, kb, h * 128:(h + 1) * 128],
                                               rhs=qT[c * 64:(c + 1) * 64, h, col0:W], start=True, stop=True), r=["kvb%d" % kb, "qT"], w=["st%d" % sb_])
                if near:
                    D0 = W * t - 128 * kb
                    jj0 = D0 + col0
                    nbc = min(256 - jj0, W - col0)
                    at_ = atmp[sb_]
                    S.add("dve", lambda e: e.scalar_tensor_tensor(out=at_[:, col0:col0 + nbc], in0=stp[sb_][:, col0:col0 + nbc], scalar=0.125,
                                                                  in1=Mb[:, h, jj0:jj0 + nbc], op0=ALU.mult, op1=ALU.add),
                          r=["st%d" % sb_, "Mb"], w=["stg%d" % sb_])
                    S.add("act", lambda e: e.activation(out=pT[pb][:, col0:col0 + nbc], in_=at_[:, col0:col0 + nbc], func=AF.Exp),
                          r=["stg%d" % sb_], w=["pT%d" % pb])
                    if col0 + nbc < W:
                        S.add("act", lambda e: e.activation(out=pT[pb][:, col0 + nbc:W], in_=stp[sb_][:, col0 + nbc:W], func=AF.Exp, scale=0.125),
                              r=["st%d" % sb_, "pT%d" % pb], w=["pT%d" % pb])
                else:
                    S.add("act", lambda e: e.activation(out=pT[pb][:, col0:W], in_=stp[sb_][:, col0:W], func=AF.Exp, scale=0.125),
                          r=["st%d" % sb_], w=["pT%d" % pb])

            def emit_PV(i):
                h, c, kb = blocks[i]
                m = kb - 4 * t
                col0 = 128 * m if m >= 1 else 0
                pb = i % 3
                aO, aOk = accOs[c]
                aL, aLk = accLs[c]
                S.add("pe", lambda e: e.matmul(aO[:, col0:W], lhsT=KVb[:, kb, 512 + h * 128:512 + (h + 1) * 128], rhs=pT[pb][:, col0:W],
                                               start=(kb == 0), stop=(kb == nkb - 1)), r=["kvb%d" % kb, "pT%d" % pb], w=[aOk])
                S.add("pe", lambda e: e.matmul(aL[:, col0:W], lhsT=ones_b[:, :], rhs=pT[pb][:, col0:W],
                                               start=(kb == 0), stop=(kb == nkb - 1)), r=["ones_b", "pT%d" % pb], w=[aLk])
                if kb == nkb - 1:
                    S.add("dve", lambda e: e.reciprocal(out=rl[:, 0:W], in_=aL[:, 0:W]), r=[aLk], w=["rl"])
                    S.add("dve", lambda e: e.tensor_tensor(out=On[c][:, 0:W], in0=aO[:, 0:W], in1=rl[:, 0:W], op=ALU.mult),
                          r=[aOk, "rl"], w=["On%d" % c])
                    if c == 1:
                        head_norm(h)

            LOOK = 2
            for i in range(min(LOOK, len(blocks))):
                emit_S(i)
            for i in range(len(blocks)):
                emit_PV(i)
                if i + LOOK < len(blocks):
                    emit_S(i + LOOK)
                bg_site(t, 0.7)
        oSrc = oTs if samp else oT
        okey = (lambda h: "oTs") if samp else (lambda h: "mT%d" % h)
        for h in range(4):
            sumsq_rstd(lambda k, h=h: oSrc[:, h, 0:Wt], [okey(h)], 1, 128.0)
            S.add("dve", lambda e, h=h: e.tensor_tensor(out=tmpA[0][:, 0:Wt], in0=oSrc[:, h, 0:Wt], in1=rstd[:, 0:Wt], op=ALU.mult),
                  r=[okey(h), "rstd"], w=["tmpA0"])
            S.add("act", lambda e, h=h: e.activation(out=mixin[:, 4 + h, 0:Wt], in_=tmpA[0][:, 0:Wt], func=AF.Copy, scale=ghs[:, 0:1]),
                  r=["tmpA0", "ghs"], w=["mixin%d" % (4 + h)])

        if SUB < 6:
            return
        if (not samp) and t == 0:
            bgsched["slots"] = []
            bgsched["inflight"] = []
        if (not samp) and t == 3:
            bg_begin(3)
        mkeys = ["mixin%d" % k for k in range(KC)]
        for blk in range(2):
            wi = load_w(w_out_d[blk], 4 + blk)
            for j in range(4):
                oc = blk * 4 + j
                b = mmi[0] % 2
                mmi[0] += 1
                def f(e, j=j, b=b, wi=wi):
                    last = None
                    for kc in range(KC):
                        last = e.matmul(mm[b][:, 0:Wt], lhsT=wbuf[wi][:, kc, j * 128:(j + 1) * 128], rhs=mixin[:, kc, 0:Wt], start=(kc == 0), stop=(kc == KC - 1))
                    return last
                S.add("pe", f, r=["wbuf%d" % wi] + mkeys, w=["mm%d" % b])
                flush_sq(0)
                evac_mix(b, oc, "G1")
                if not samp:
                    bg_site(t)
        post_norm_res("G1")

        if SUB < 7:
            return
        pre_norm("A2", 24)
        SL2 = 2 + L
        upi = [0]
        for grp in range(11):
            ui = load_w(w_up_d[grp].rearrange("p a k c -> p (a k c)").rearrange("p (k c) -> p k c", k=KC), 6 + grp)
            wupv = wbuf[ui][:].rearrange("p k c -> p (k c)").rearrange("p (a k c) -> p a k c", a=2, k=KC)
            for pr in range(2):
                i = grp * 2 + pr
                ub = i % 2
                Uv = U[ub][:, 0:2 * nseg * SL2].rearrange("p (a s l) -> p a s l", a=2, l=SL2)
                Yv = yv[ub][:, 0:2 * Wt].rearrange("p (a s l) -> p a s l", a=2, l=L)
                for a in range(2):
                    ch = a * NFC + i
                    b = mmi[0] % 2
                    mmi[0] += 1
                    def f(e, a=a, pr=pr, b=b, wupv=wupv):
                        last = None
                        for kc in range(KC):
                            last = e.matmul(mm[b][:, 0:Wt], lhsT=wupv[:, pr, kc, a * 128:(a + 1) * 128], rhs=hT[:, kc, 0:Wt], start=(kc == 0), stop=(kc == KC - 1))
                        return last
                    S.add("pe", f, r=["wbuf%d" % ui] + hkeys, w=["mm%d" % b])
                    if not samp:
                        bg_site(t)
                    S.add("act", lambda e, ch=ch, a=a, Uv=Uv: e.activation(out=Uv[:, a, :, 0:2], in_=halo[:, ch, 0:nseg, :], func=AF.Copy), r=["halo%d" % ch], w=["U%d_%d" % (ub, a)])
                    S.add("act", lambda e, a=a, b=b, Uv=Uv: e.activation(out=Uv[:, a, :, 2:SL2], in_=seg(mm[b][:, 0:Wt]), func=AF.Copy),
                          r=["mm%d" % b, "U%d_%d" % (ub, a)], w=["U%d_%d" % (ub, a)])
                    S.add("act", lambda e, ch=ch, a=a, Uv=Uv: e.activation(out=halo[:, ch, 0:nseg, :], in_=Uv[:, a, :, L:L + 2], func=AF.Copy),
                          r=["U%d_%d" % (ub, a)], w=["halo%d" % ch])
                    S.add("act", lambda e, ch=ch, a=a, Uv=Uv, Yv=Yv: e.activation(out=Yv[:, a, :, :], in_=Uv[:, a, :, 2:SL2], func=AF.Identity,
                                                                               scale=P("cw2", 1, ch), bias=P("cb", 1, ch)),
                          r=["U%d_%d" % (ub, a), "params"], w=["yv%d_%d" % (ub, a)])
                    S.add("dve", lambda e, ch=ch, a=a, Uv=Uv, Yv=Yv: e.scalar_tensor_tensor(out=Yv[:, a, :, :], in0=Uv[:, a, :, 1:1 + L], scalar=P("cw1", 1, ch),
                                                                                          in1=Yv[:, a, :, :], op0=ALU.mult, op1=ALU.add),
                          r=["U%d_%d" % (ub, a), "yv%d_%d" % (ub, a), "params"], w=["yv%d_%d" % (ub, a)])
                    S.add("dve", lambda e, ch=ch, a=a, Uv=Uv, Yv=Yv: e.scalar_tensor_tensor(out=Yv[:, a, :, :], in0=Uv[:, a, :, 0:L], scalar=P("cw0", 1, ch),
                                                                                          in1=Yv[:, a, :, :], op0=ALU.mult, op1=ALU.add),
                          r=["U%d_%d" % (ub, a), "yv%d_%d" % (ub, a), "params"], w=["yv%d_%d" % (ub, a)])
                S.add("act", lambda e, ub=ub: e.activation(out=yv[ub][:, 0:Wt], in_=yv[ub][:, 0:Wt], func=AF.Gelu_apprx_tanh),
                      r=["yv%d_0" % ub], w=["yv%d_0" % ub])
                S.add("dve", lambda e, ub=ub, i=i: e.tensor_tensor(out=gT[:, i, 0:Wt], in0=yv[ub][:, 0:Wt], in1=yv[ub][:, Wt:2 * Wt], op=ALU.mult),
                      r=["yv%d_0" % ub, "yv%d_1" % ub], w=["gT%d" % i])
        if samp:
            sp_dma(lambda e: e.dma_start(out=conv_so.rearrange("p a s l -> p (a s l)"), in_=halo[:].rearrange("p a s l -> p (a s l)")), r=["halo%d" % c_ for c_ in range(44)], out=True)
        elif t == NT - 1:
            sp_dma(lambda e: e.dma_start(out=conv_po, in_=halo[:, :, 0, :]), r=["halo%d" % c_ for c_ in range(44)], out=True)
        if SUB < 8:
            return
        gkeys = ["gT%d" % i for i in range(NFC)]
        dni = [0]
        for blk in range(8):
            di = load_w(w_down_d[blk], 17 + blk)
            wdv = wbuf[di][:].rearrange("p k c -> p (k c)")
            for j in range(1):
                oc = blk
                b = mmi[0] % 2
                mmi[0] += 1
                def f(e, b=b, wdv=wdv):
                    last = None
                    for i in range(NFC):
                        last = e.matmul(mm[b][:, 0:Wt], lhsT=wdv[:, i * 128:(i + 1) * 128], rhs=gT[:, i, 0:Wt], start=(i == 0), stop=(i == NFC - 1))
                    return last
                S.add("pe", f, r=["wbuf%d" % di] + gkeys, w=["mm%d" % b])
                flush_sq(0)
                evac_mix(b, oc, "G2")
                if not samp:
                    bg_site(t, 4.8)
        if SUB < 9:
            return
        post_norm_res("G2")
        if SUB < 10:
            return
        dst = yT_s if samp else yT_p[:, t * W:(t + 1) * W]
        for k in range(KC):
            sp_dma(lambda e, k=k: e.dma_start(out=dst[k * 128:(k + 1) * 128, :], in_=xT[:, k, 0:Wt]), r=["xT%d" % k], out=True)

    return nc, S, es, run_tile, mod_rest, uT, sconv_d


def emit_program(nc, S, es):
    sems = {}

    def semof(k):
        if k not in sems:
            sems[k] = es.enter_context(nc.semaphore("s_" + "_".join(str(x) for x in k)))
        return sems[k]
    for e in ENG:
        semof(("e", e))
    for q in ("sp", "pool"):
        for i in range(min(NDS, S.dma_i[q])):
            semof(("d", q, i))
    block = es.enter_context(nc.Block())

    def emit(name, eng):
        for waits, fn, tok in S.ops[name]:
            for sk, v in waits:
                eng.wait_ge(semof(sk), v)
            ins = fn(eng)
            sk, v = tok
            ins.then_inc(semof(sk), 16 if sk[0] == "d" else 1)
        if name == "sp":
            for sk, v in S.out_tokens:
                eng.wait_ge(semof(sk), v)

    @block.tensor
    def _(e):
        emit("pe", e)

    @block.scalar
    def _(e):
        emit("act", e)

    @block.vector
    def _(e):
        emit("dve", e)

    @block.gpsimd
    def _(e):
        emit("pool", e)

    @block.sync
    def _(e):
        emit("sp", e)


_CACHE = {}
_DEBUG_CORES = None


def get_program():
    if "nc" not in _CACHE:
        nc, S, es, run_tile, mod_rest, uT, sconv_d = build_program()
        run_tile(True, 0, "A")
        mod_rest()
        for t in range(NT):
            run_tile(False, t)
        run_tile(True, 0, "B")
        emit_program(nc, S, es)
        es.close()
        _CACHE["nc"] = nc
    return _CACHE["nc"]


def _bucket_thresholds():
    d = np.arange(0, 4096)
    dd = np.maximum(d, 1).astype(np.float32)
    large = 16 + (np.log(dd / np.float32(16)) / np.float32(math.log(128 / 16)) * np.float32(16)).astype(np.int32)
    large = np.minimum(large, 31)
    return np.where(d < 16, d, large)


def kernel(x_prompt, x_sample, c_prompt, c_sample, cache_k, cache_v, page_table, state_pool, state_conv,
           w_ada, b_ada, g_pre_mix, g_post_mix, g_pre_ffn, g_post_ffn, w_in, w_out, w_pool, pool_scale,
           lam_q1, lam_k1, lam_q2, lam_k2, g_head, rel_bias, w_up, conv_w, conv_b, w_down):
    f32 = np.float32
    A = lambda a: np.ascontiguousarray(np.asarray(a))
    x_prompt = A(x_prompt); x_sample = A(x_sample)
    def fm(v, n):
        return A(np.asarray(v, f32).reshape(n, 128).T)
    par = np.zeros((128, NPAR), f32)
    par[:, PO["g_pre_mix"]:PO["g_pre_mix"] + 8] = fm(g_pre_mix[0], 8)
    par[:, PO["g_post_mix"]:PO["g_post_mix"] + 8] = fm(g_post_mix[0], 8)
    par[:, PO["g_pre_ffn"]:PO["g_pre_ffn"] + 8] = fm(g_pre_ffn[0], 8)
    par[:, PO["g_post_ffn"]:PO["g_post_ffn"] + 8] = fm(g_post_ffn[0], 8)
    par[:, PO["b_ada"]:PO["b_ada"] + 48] = fm(b_ada[0], 48)
    par[:, PO["pool_scale"]:PO["pool_scale"] + 4] = fm(pool_scale[0], 4)
    par[:, PO["g_head"]:PO["g_head"] + 1] = fm(g_head[0], 1)
    cw = np.asarray(conv_w[0], f32)
    par[:, PO["cw0"]:PO["cw0"] + 44] = fm(cw[0], 44)
    par[:, PO["cw1"]:PO["cw1"] + 44] = fm(cw[1], 44)
    par[:, PO["cw2"]:PO["cw2"] + 44] = fm(cw[2], 44)
    par[:, PO["cb"]:PO["cb"] + 44] = fm(conv_b[0], 44)
    for g in range(4):
        wwin = 2 ** (g + 1)
        par[:, PO["rc"] + g * 16:PO["rc"] + (g + 1) * 16] = (1.0 / np.minimum(np.arange(16) + 1, wwin)).astype(f32)[None, :]
    w_ada_l = A(np.asarray(w_ada[0], f32).reshape(KC, 128, 12, 512).transpose(2, 1, 0, 3))
    w_in_l = A(np.asarray(w_in[0], f32).reshape(KC, 128, 4, 512).transpose(2, 1, 0, 3))
    w_out_l = A(np.asarray(w_out[0], f32).reshape(KC, 128, 2, 512).transpose(2, 1, 0, 3))
    w_pool_l = A(np.asarray(w_pool[0], f32).transpose(1, 0, 2))
    wu = np.asarray(w_up[0], f32).reshape(KC, 128, 2, 11, 2, 128)
    w_up_l = A(wu.transpose(3, 1, 4, 0, 2, 5).reshape(11, 128, 2, KC, 256))
    wd = np.asarray(w_down[0], f32).reshape(NFC, 128, 8, 128)
    w_down_l = np.zeros((8, 128, KC * 512), f32)
    w_down_l[:, :, 0:NFC * 128] = wd.transpose(2, 1, 0, 3).reshape(8, 128, NFC * 128)
    w_down_l = w_down_l.reshape(8, 128, KC, 512)
    lamv = A(np.concatenate([lam_q1[0], lam_k1[0], lam_q2[0], lam_k2[0]]).astype(f32)[None, :])
    bk = _bucket_thresholds()
    oneh = np.zeros((33, FL), f32)
    for i in range(FL):
        d = i - 511
        if d < 0:
            oneh[32, i] = 1.0
        else:
            oneh[bk[d], i] += 1.0
            oneh[31, i] -= 1.0
    jmat = A(np.eye(128, dtype=f32)[::-1])
    kcl = A(np.asarray(cache_k[0], f32).transpose(0, 3, 4, 2, 1).reshape(N_PHYS, 128, 4, 128)).reshape(N_PHYS * 128, 512)
    vcl = np.asarray(cache_v[0], f32).reshape(N_PHYS * 128, 512)
    kvl = np.concatenate([kcl, vcl], axis=1)
    del kcl, vcl
    def core_map(i):
        sq = slice(i * NS, (i + 1) * NS)
        cTi = np.concatenate([np.asarray(c_sample[sq], f32), np.asarray(c_prompt[i:i + 1], f32)], axis=0)
        return {
            "xT_p": A(x_prompt[i].T), "xT_s": A(x_sample[sq].reshape(WS, D).T),
            "cT": A(cTi.reshape(17, KC, 128).transpose(2, 1, 0)), "params": par,
            "w_ada_l": w_ada_l, "w_in_l": w_in_l, "w_out_l": w_out_l, "w_pool_l": w_pool_l,
            "w_up_l": w_up_l, "w_down_l": w_down_l, "lamv": lamv, "rel_bias": A(np.asarray(rel_bias, f32)),
            "onehot": oneh, "jmat": jmat, "kvcache_l": kvl,
            "pt": A(np.asarray(page_table[sq], np.int32).reshape(1, NS * NPG)),
            "spool_l": A(np.asarray(state_pool[0, sq], f32).reshape(NS, 15, 4, 128).transpose(3, 2, 0, 1)),
            "sconv_l": A(np.asarray(state_conv[0, sq], f32).reshape(NS, 2, 44, 128).transpose(3, 2, 0, 1)),
        }
    if _DEBUG_CORES is not None:
        return core_map
    in_maps = [core_map(i) for i in range(NCORES)]
    nc = get_program()
    res = run_bass_kernel_spmd(nc, in_maps, core_ids=list(range(NCORES)))
    R = res.results
    y_p = np.stack([R[i]["yT_p"].T for i in range(NCORES)])
    y_s = np.concatenate([R[i]["yT_s"].T.reshape(NS, LS, D) for i in range(NCORES)])
    k_p = np.stack([R[i]["kT_p"].T.reshape(T, 4, 2, 64) for i in range(NCORES)])[None]
    v_p = np.stack([R[i]["v_p"].reshape(T, 4, 128) for i in range(NCORES)])[None]
    pool_p = np.stack([R[i]["pool_p"].transpose(2, 1, 0).reshape(15, 512) for i in range(NCORES)])[None]
    conv_p = np.stack([R[i]["conv_p"].transpose(2, 1, 0).reshape(2, 2 * DFF) for i in range(NCORES)])[None]
    k_s = np.concatenate([R[i]["kT_s"].T.reshape(NS, LS, 4, 2, 64) for i in range(NCORES)])[None]
    v_s = np.concatenate([R[i]["v_s"].reshape(NS, LS, 4, 128) for i in range(NCORES)])[None]
    pool_s = np.concatenate([R[i]["pool_s"].transpose(2, 3, 1, 0).reshape(NS, 15, 512) for i in range(NCORES)])[None]
    conv_s = np.concatenate([R[i]["conv_s"].transpose(2, 3, 1, 0).reshape(NS, 2, 2 * DFF) for i in range(NCORES)])[None]
    outs = (y_p, y_s, k_p, v_p, pool_p, conv_p, k_s, v_s, pool_s, conv_s)
    return tuple(np.ascontiguousarray(o, dtype=np.float32) for o in outs)
```

```python
import math
import numpy as np
import concourse.bass as bass
import concourse.mybir as mybir
from concourse.bass_utils import run_bass_kernel_spmd

F32 = mybir.dt.float32
BF16 = mybir.dt.bfloat16
I32 = mybir.dt.int32
AF = mybir.ActivationFunctionType
ALU = mybir.AluOpType

NCORES = 8
D = 1024
KC = 8
T = 2048
W = 512
NT = T // W
NS = 16
LS = 8
WS = NS * LS
NPG = 16
DFF = 2816
NFC = 22
EPS = 1e-6
LAM_INIT = 0.2
NEG = -10000.0
N_PHYS = 2560
STAGE = 99
SUB = 99
MICRO = 99
BG_US = {0: 5.0, 1: 6.0, 2: 10.0, 3: 5.0}
FL = 1152

PO = {}
_o = 0
for _n, _c in [("g_pre_mix", 8), ("g_post_mix", 8), ("g_pre_ffn", 8), ("g_post_ffn", 8), ("b_ada", 48),
               ("pool_scale", 4), ("g_head", 1), ("cw0", 44), ("cw1", 44), ("cw2", 44), ("cb", 44), ("rc", 64)]:
    PO[_n] = _o
    _o += _c
NPAR = _o

ENG = ["pe", "act", "dve", "pool", "sp"]
NDS = 40


class Sched:
    def __init__(self):
        self.ops = {e: [] for e in ENG}
        self.cnt = {e: 0 for e in ENG}
        self.lw = {}
        self.rd = {}
        self.known = {e: {} for e in ENG}
        self.dma_i = {"sp": 0, "pool": 0, "act": 0}
        self.out_tokens = []

    PSUM_KEYS = ("mm0", "mm1", "ssp", "st0", "st1", "accO", "accL", "misc", "fb0", "fb1", "fb2")

    def add(self, eng, fn, r=(), w=(), dma=False, out=False):
        w = list(w) + [k for k in r if k in self.PSUM_KEYS]
        r = [k for k in r if k not in self.PSUM_KEYS]
        deps = set()
        for b in r:
            if b in self.lw:
                deps.add(self.lw[b])
        for b in w:
            if b in self.lw:
                deps.add(self.lw[b])
            for t in self.rd.get(b, ()):
                deps.add(t)
        if dma:
            i = self.dma_i[eng]
            self.dma_i[eng] += 1
            k = i % NDS
            v = 16 * (i // NDS + 1)
            tok = (("d", eng, k), v)
            if v > 16:
                deps.add((("d", eng, k), v - 16))
        else:
            self.cnt[eng] += 1
            tok = (("e", eng), self.cnt[eng])
        kn = self.known[eng]
        mx = {}
        for (s, v) in deps:
            if s == ("e", "pe") and eng == "pe":
                continue
            if kn.get(s, 0) >= v:
                continue
            mx[s] = max(mx.get(s, 0), v)
        for s, v in mx.items():
            kn[s] = v
        self.ops[eng].append((list(mx.items()), fn, tok))
        for b in w:
            self.lw[b] = tok
            self.rd[b] = []
        for b in r:
            self.rd.setdefault(b, []).append(tok)
        if out:
            self.out_tokens.append(tok)
        return tok


def build_program():
    nc = bass.Bass("TRN2", target_bir_lowering=False)
    S = Sched()

    def din(name, shape, dt=F32):
        return nc.dram_tensor(name, list(shape), dt, kind="ExternalInput").ap()

    def dout(name, shape, dt=F32):
        return nc.dram_tensor(name, list(shape), dt, kind="ExternalOutput").ap()

    xT_p = din("xT_p", [D, T]); xT_s = din("xT_s", [D, WS])
    cT_d = din("cT", [128, KC, 17]); params_d = din("params", [128, NPAR])
    w_ada_d = din("w_ada_l", [12, 128, KC, 512]); w_in_d = din("w_in_l", [4, 128, KC, 512])
    w_out_d = din("w_out_l", [2, 128, KC, 512]); w_pool_d = din("w_pool_l", [128, 4, 128])
    w_up_d = din("w_up_l", [11, 128, 2, KC, 256]); w_down_d = din("w_down_l", [8, 128, KC, 512])
    lamv_d = din("lamv", [1, 256]); rb_d = din("rel_bias", [32, 4]); oneh_d = din("onehot", [33, FL])
    jmat_d = din("jmat", [128, 128])
    kv_d = din("kvcache_l", [N_PHYS * 128, 1024])
    pt_d = din("pt", [1, NS * NPG], I32)
    spool_d = din("spool_l", [128, 4, NS, 15]); sconv_d = din("sconv_l", [128, 44, NS, 2])

    yT_p = dout("yT_p", [D, T]); yT_s = dout("yT_s", [D, WS])
    kT_po = dout("kT_p", [512, T]); v_po = dout("v_p", [T, 512])
    pool_po = dout("pool_p", [128, 4, 15]); conv_po = dout("conv_p", [128, 44, 2])
    kT_so = dout("kT_s", [512, WS]); v_so = dout("v_s", [WS, 512])
    pool_so = dout("pool_s", [128, 4, NS, 15]); conv_so = dout("conv_s", [128, 44, NS, 2])
    fscr = nc.dram_tensor("fscr", [4, FL], F32, kind="Internal").ap()
    wscr = nc.dram_tensor("wscr", [25, 128, KC * 512], BF16, kind="Internal").ap()
    vscr = nc.dram_tensor("vscr", [NS, 8, 512], BF16, kind="Internal").ap()

    from contextlib import ExitStack
    es = ExitStack()

    def sb(name, shape, dt=F32):
        return es.enter_context(nc.sbuf_tensor("sb_" + name, list(shape), dt))

    def ps(name, dt=F32, cols=512):
        return es.enter_context(nc.psum_tensor("ps_" + name, [128, cols], dt))

    params = sb("params", [128, NPAR]); cT = sb("cT", [128, KC, 17]); cTs = sb("cTs", [128, KC, 17], BF16)
    modT = sb("modT", [128, 48, 17])
    MA = {k: sb("M" + k, [128, KC, 17]) for k in ["A1", "G1", "A2", "G2"]}
    ones_b = sb("ones_b", [128, 128], BF16)
    lamt = sb("lamt", [128, 256]); lamw = sb("lamw", [128, 8]); neglam = sb("neglam", [128, 1])
    ghs = sb("ghs", [128, 1])
    rbx = sb("rbx", [33, 4]); jmat = sb("jmat", [128, 128])
    Mb = sb("Mb", [128, 4, 256])
    nb1 = sb("nb1", [128, 4, 2, 8]); nb2 = sb("nb2", [8, 4, 2, 8])
    wpool = sb("wpool", [128, 4, 128], BF16)
    NWB = 3
    wbuf = [sb("wbuf%d" % i, [128, KC, 512], BF16) for i in range(NWB)]
    epsc = sb("epsc", [128, 1])
    xT = sb("xT", [128, KC, W]); hT = sb("hT", [128, KC, W], BF16)
    sqb = [sb("sqb%d" % i, [128, W], BF16) for i in range(2)]
    rstd = sb("rstd", [128, W]); tmpA = [sb("tmpA%d" % i, [128, W]) for i in range(2)]
    qT = sb("qT", [128, 4, W], BF16)
    KVb = sb("KVb", [128, T // 128, 1024], BF16)
    kTs = sb("kTs", [128, 4, WS], BF16)
    stg = [sb("stg%d" % i, [128, 512]) for i in range(2)]
    uT = sb("uT", [128, 4, (16 + W)])
    dT = sb("dT", [128, 4, W], BF16); mixin = sb("mixin", [128, KC, W], BF16)
    mT = sb("mT", [128, KC, W])
    pT = [sb("pT%d" % i, [128, W], BF16) for i in range(3)]
    atmp = stg
    oT = mT[:, 0:4, :]
    U = [sb("U%d" % i, [128, 2 * (2 + W)]) for i in range(2)]
    yv = [sb("yv%d" % i, [128, 2 * W]) for i in range(2)]
    pw = [U[i][:, 0:16 + W] for i in range(2)]
    rl = yv[0][:, 0:W]; On = [yv[0][:, W:2 * W], yv[1][:, 0:W]]
    halo = sb("halo", [128, 44, NS, 2])
    gT = sb("gT", [128, NFC, W], BF16)
    gflat = gT[:].rearrange("p a b -> p (a b)")
    oneh = gflat[0:33, 0:2 * FL].bitcast(F32)
    fsb = gflat[0:4, 2 * FL:4 * FL].bitcast(F32)
    vnew = gT[0:8, 0:NS, :]
    ptb = sb("ptb", [128, NS * NPG], I32); iop = sb("iop", [128, 1], I32); iof = sb("iof", [128, 1])
    idx = sb("idx", [128, NS * NPG], I32)
    NKR = 16
    kpg = [KVb[:, i, 0:512] for i in range(NKR)]
    vpg = [KVb[:, i, 512:1024] for i in range(NKR)]
    pTs = sb("pTs", [128, NPG, 64], BF16); pnew = sb("pnew", [8, 64], BF16)
    qbd = sb("qbd", [128, 4, NS, 16], BF16); oTs = sb("oTs", [128, 4, WS]); mixs = sb("mixs", [128, 4, WS], BF16)
    vnb = sb("vnb", [8, 2, 512], BF16)
    bgt = sb("bgt", [128, 384])

    mm = [ps("mm0"), ps("mm1")]; ssp = ps("ssp"); stp = [ps("st0"), ps("st1")]
    accO = ps("accO"); accL = ps("accL"); misc = ps("misc")

    sp_dma = lambda fn, r=(), w=(), out=False: S.add("sp", fn, r, w, dma=True, out=out)
    pool_dma = lambda fn, r=(), w=(), out=False: S.add("pool", fn, r, w, dma=True, out=out)

    def P(name, n=1, c=0):
        o = PO[name] + c
        return params[:, o:o + n]

    sp_dma(lambda e: e.dma_start(out=params[:], in_=params_d), w=["params"])
    sp_dma(lambda e: e.dma_start(out=cT[:], in_=cT_d), w=["cT"])
    sp_dma(lambda e: e.dma_start(out=lamt[:], in_=lamv_d.partition_broadcast(128)), w=["lamt"])
    sp_dma(lambda e: e.dma_start(out=rbx[0:32, :], in_=rb_d), w=["rbx0"])
    sp_dma(lambda e: e.dma_start(out=oneh[:], in_=oneh_d), w=["oneh"])
    sp_dma(lambda e: e.dma_start(out=jmat[:], in_=jmat_d), w=["jmat"])
    sp_dma(lambda e: e.dma_start(out=ptb[:], in_=pt_d.partition_broadcast(128)), w=["ptb"])
    pool_dma(lambda e: e.dma_start(out=wpool[:], in_=w_pool_d), w=["wpool"])
    S.add("dve", lambda e: e.memset(ones_b[:], 1.0), w=["ones_b"])
    S.add("dve", lambda e: e.memset(qbd[:], 0.0), w=["qbd"])
    S.add("dve", lambda e: e.memset(epsc[:], EPS), w=["epsc"])
    S.add("dve", lambda e: e.memset(rbx[32:33, :], NEG), w=["rbx1"])
    S.add("dve", lambda e: e.memset(halo[:], 0.0), w=["halo%d" % c_ for c_ in range(44)])
    S.add("dve", lambda e: e.memset(uT[:], 0.0), w=["uT"])
    S.add("pool", lambda e: e.iota(iop[:], pattern=[[0, 1]], base=0, channel_multiplier=1), w=["iop"])
    S.add("pool", lambda e: e.tensor_copy(out=iof[:], in_=iop[:]), r=["iop"], w=["iof"])
    S.add("pool", lambda e: e.tensor_scalar(out=idx[:], in0=ptb[:], scalar1=128.0, scalar2=iof[:, 0:1],
                                            op0=ALU.mult, op1=ALU.add), r=["ptb", "iof"], w=["idx"])
    S.add("dve", lambda e: e.tensor_tensor(out=lamt[:, 0:64], in0=lamt[:, 0:64], in1=lamt[:, 64:128], op=ALU.mult),
          r=["lamt"], w=["lamt"])
    S.add("dve", lambda e: e.tensor_tensor(out=lamt[:, 128:192], in0=lamt[:, 128:192], in1=lamt[:, 192:256], op=ALU.mult),
          r=["lamt"], w=["lamt"])
    S.add("dve", lambda e: e.reduce_sum(out=lamw[:, 0:1], in_=lamt[:, 0:64], axis=mybir.AxisListType.X), r=["lamt"], w=["lamw0"])
    S.add("dve", lambda e: e.reduce_sum(out=lamw[:, 1:2], in_=lamt[:, 128:192], axis=mybir.AxisListType.X), r=["lamt"], w=["lamw1"])
    S.add("act", lambda e: e.activation(out=lamw[:, 2:4], in_=lamw[:, 0:2], func=AF.Exp), r=["lamw0", "lamw1"], w=["lamw2"])
    S.add("dve", lambda e: e.tensor_tensor(out=lamw[:, 4:5], in0=lamw[:, 3:4], in1=lamw[:, 2:3], op=ALU.subtract), r=["lamw2"], w=["lamw4"])
    S.add("dve", lambda e: e.tensor_scalar(out=neglam[:], in0=lamw[:, 4:5], scalar1=-LAM_INIT, scalar2=None, op0=ALU.add),
          r=["lamw4"], w=["neglam"])
    S.add("dve", lambda e: e.tensor_scalar(out=ghs[:], in0=P("g_head"), scalar1=(1.0 - LAM_INIT), scalar2=None, op0=ALU.mult),
          r=["params"], w=["ghs"])

    fbanks = [mm[0], mm[1], ssp]
    for i in range(3):
        S.add("pe", lambda e, i=i: e.matmul(fbanks[i][0:4, 0:384], lhsT=rbx[:, :], rhs=oneh[:, i * 384:(i + 1) * 384], start=True, stop=True),
              r=["rbx0", "rbx1", "oneh"], w=["fb%d" % i])
        S.add("act", lambda e, i=i: e.activation(out=fsb[:, i * 384:(i + 1) * 384], in_=fbanks[i][0:4, 0:384], func=AF.Copy),
              r=["fb%d" % i], w=["fsb%d" % i])
    sp_dma(lambda e: e.dma_start(out=fscr, in_=fsb[:]), r=["fsb0", "fsb1", "fsb2"], w=["fscr"])
    Mpv = mT[:, 0:2, :].rearrange("p a b -> p (a b)").rearrange("p (h j) -> p h j", h=4)
    for h in range(4):
        sp_dma(lambda e, h=h: e.dma_start(out=Mpv[:, h, :], in_=bass.AP(tensor=fscr.tensor, offset=h * FL + 384, ap=[[1, 128], [1, 256]])),
               r=["fscr"], w=["Mp%d" % h])
    for h in range(4):
        bk = mm[h % 2]
        S.add("pe", lambda e, h=h, bk=bk: e.matmul(bk[:, 0:256], lhsT=jmat[:, :], rhs=Mpv[:, h, :], start=True, stop=True),
              r=["jmat", "Mp%d" % h], w=["mm%d" % (h % 2)])
        S.add("act", lambda e, h=h, bk=bk: e.activation(out=Mb[:, h, :], in_=bk[:, 0:256], func=AF.Copy),
              r=["mm%d" % (h % 2)], w=["Mb"])
    for h in range(4):
        for c in range(2):
            S.add("dve", lambda e, h=h, c=c: e.tensor_copy(out=nb1[:, h, c, :], in_=Mb[:, h, 128:136]), r=["Mb"], w=["nb1"])
            S.add("dve", lambda e, h=h, c=c: e.tensor_copy(out=nb2[:, h, c, :], in_=Mb[0:8, h, 0:8]), r=["Mb"], w=["nb2"])


    S.add("act", lambda e: e.activation(out=cTs[:], in_=cT[:], func=AF.Silu), r=["cT"], w=["cTs"])
    def mod_blocks(b0, b1):
        for blk in range(b0, b1):
            wb = wbuf[blk % 2]
            pool_dma(lambda e, blk=blk, wb=wb: e.dma_start(out=wb[:], in_=w_ada_d[blk]), w=["wbuf%d" % (blk % 2)])
            bk = mm[blk % 2]
            def mm_ada(e, wb=wb, bk=bk):
                last = None
                for j in range(4):
                    for kc in range(KC):
                        last = e.matmul(bk[:, j * 17:(j + 1) * 17], lhsT=wb[:, kc, j * 128:(j + 1) * 128], rhs=cTs[:, kc, :],
                                        start=(kc == 0), stop=(kc == KC - 1))
                return last
            S.add("pe", mm_ada, r=["wbuf%d" % (blk % 2), "cTs"], w=["mm%d" % (blk % 2)])
            S.add("dve", lambda e, blk=blk, bk=bk: e.tensor_tensor(
                out=modT[:, blk * 4:(blk + 1) * 4, :], in0=bk[:, 0:68].rearrange("p (j s) -> p j s", s=17),
                in1=P("b_ada", 4, blk * 4).unsqueeze(2).to_broadcast([128, 4, 17]), op=ALU.add),
                r=["mm%d" % (blk % 2), "params"], w=["modT%d" % blk])
    def derive(name, mo, gname, plus1):
        gb = P(gname, 8).unsqueeze(2).to_broadcast([128, KC, 17])
        mk = ["modT%d" % (mo // 4), "modT%d" % (mo // 4 + 1)]
        if plus1:
            S.add("dve", lambda e: e.scalar_tensor_tensor(out=MA[name][:], in0=modT[:, mo:mo + 8, :], scalar=1.0, in1=gb,
                                                          op0=ALU.add, op1=ALU.mult), r=mk + ["params"], w=["M" + name])
        else:
            S.add("dve", lambda e: e.tensor_tensor(out=MA[name][:], in0=modT[:, mo:mo + 8, :], in1=gb, op=ALU.mult),
                  r=mk + ["params"], w=["M" + name])
    mod_blocks(0, 4)
    derive("A1", 8, "g_pre_mix", True)

    def mod_rest():
        mod_blocks(4, 12)
        derive("G1", 16, "g_post_mix", False)
        derive("A2", 32, "g_pre_ffn", True); derive("G2", 40, "g_post_ffn", False)

    nmm = [0]
    converted = set()
    pref = {}
    bgq = []
    bgstate = {"site": 0}


    def build_bg_units():
        for s in range(NS):
            for q in range(4):
                bgq.append(("pages", s, q))
            bgq.append(("final", s, 0))

    bgsched = {"tile": -1, "slots": [], "k": 0, "pending_gather": []}

    def bg_gather(s, q, slot):
        for p in range(4):
            n = s * NPG + q * 4 + p
            blk = slot * 4 + p
            pool_dma(lambda e, n=n, blk=blk: e.indirect_dma_start(out=KVb[:, blk, :], out_offset=None, in_=kv_d,
                     in_offset=bass.IndirectOffsetOnAxis(ap=idx[:, n:n + 1], axis=0)), r=["idx"], w=["kvb%d" % blk])

    def bg_pages(s, q, slot):
        pbuf = (q % 4) * 4
        def sc(e):
            last = None
            for p in range(4):
                blk = slot * 4 + p
                for h in range(4):
                    o = p * 64 + h * 16
                    last = e.matmul(misc[:, o:o + 16], lhsT=KVb[:, blk, h * 128:(h + 1) * 128], rhs=qbd[:, h, s, :], start=True, stop=True,
                                    skip_group_check=True)
            return last
        bkeys = ["kvb%d" % (slot * 4 + p) for p in range(4)]
        S.add("pe", sc, r=bkeys + ["qbd"], w=["misc"])
        nfar = 3 if q == 3 else 4
        S.add("act", lambda e: e.activation(out=pTs[:, pbuf:pbuf + nfar, :], in_=misc[:, 0:nfar * 64].rearrange("p (g x) -> p g x", x=64),
                                            func=AF.Exp, scale=0.125), r=["misc"], w=["pTs"])
        if q == 3:
            S.add("dve", lambda e: e.scalar_tensor_tensor(out=bgt[:, 0:64], in0=misc[:, 192:256], scalar=0.125,
                                                          in1=nb1[:].rearrange("p h c j -> p (h c j)"), op0=ALU.mult, op1=ALU.add),
                  r=["misc", "nb1"], w=["bgt0"])
            S.add("act", lambda e: e.activation(out=pTs[:, pbuf + 3, :], in_=bgt[:, 0:64], func=AF.Exp), r=["bgt0"], w=["pTs"])
        def pv(e):
            last = None
            for p in range(4):
                blk = slot * 4 + p
                for h in range(4):
                    e.matmul(misc[:, 256 + h * 16:256 + (h + 1) * 16], lhsT=KVb[:, blk, 512 + h * 128:512 + (h + 1) * 128],
                             rhs=pTs[:, pbuf + p, h * 16:(h + 1) * 16], start=(p == 0 and h == 0), stop=False, skip_group_check=True)
                last = e.matmul(misc[:, 320:384], lhsT=ones_b[:, :], rhs=pTs[:, pbuf + p, :], start=False, stop=(p == 3), skip_group_check=True)
            return last
        S.add("pe", pv, r=bkeys + ["pTs", "ones_b"], w=["misc"])
        if q == 0:
            S.add("dve", lambda e: e.tensor_copy(out=bgt[:, 256:384], in_=misc[:, 256:384]), r=["misc"], w=["bgacc"])
        else:
            S.add("dve", lambda e: e.tensor_tensor(out=bgt[:, 256:384], in0=misc[:, 256:384], in1=bgt[:, 256:384], op=ALU.add),
                  r=["misc", "bgacc"], w=["bgacc"])

    def bg_final(s):
        sp_dma(lambda e: e.dma_start(out=vnb[:, s % 2, :], in_=vscr[s]), r=["vscr%d" % s], w=["vnb%d" % (s % 2)])
        def scn(e):
            last = None
            for h in range(4):
                last = e.matmul(misc[0:8, 384 + h * 16:384 + (h + 1) * 16], lhsT=kTs[:, h, s * 8:(s + 1) * 8], rhs=qbd[:, h, s, :],
                                start=True, stop=True, skip_group_check=True)
            return last
        S.add("pe", scn, r=["kTs", "qbd"], w=["misc"])
        S.add("dve", lambda e: e.scalar_tensor_tensor(out=bgt[0:8, 192:256], in0=misc[0:8, 384:448], scalar=0.125,
                                                      in1=nb2[:].rearrange("p h c j -> p (h c j)"), op0=ALU.mult, op1=ALU.add),
              r=["misc", "nb2"], w=["bgn"])
        S.add("act", lambda e: e.activation(out=pnew[:, :], in_=bgt[0:8, 192:256], func=AF.Exp), r=["bgn"], w=["pnew"])
        def pvn(e):
            for h in range(4):
                e.matmul(misc[:, 256 + h * 16:256 + (h + 1) * 16], lhsT=vnb[:, s % 2, h * 128:(h + 1) * 128], rhs=pnew[:, h * 16:(h + 1) * 16],
                         start=(h == 0), stop=False, skip_group_check=True)
            return e.matmul(misc[:, 320:384], lhsT=ones_b[0:8, :], rhs=pnew[:, :], start=False, stop=True, skip_group_check=True)
        S.add("pe", pvn, r=["pnew", "vnb%d" % (s % 2), "ones_b"], w=["misc"])
        S.add("dve", lambda e: e.tensor_tensor(out=bgt[:, 256:384], in0=misc[:, 256:384], in1=bgt[:, 256:384], op=ALU.add),
              r=["misc", "bgacc"], w=["bgacc"])
        S.add("dve", lambda e: e.reciprocal(out=bgt[:, 64:128], in_=bgt[:, 320:384]), r=["bgacc"], w=["bgr"])
        S.add("dve", lambda e: e.tensor_tensor(out=bgt[:, 128:192], in0=bgt[:, 256:320], in1=bgt[:, 64:128], op=ALU.mult), r=["bgacc", "bgr"], w=["bgp"])
        a4 = bgt[:, 128:192].rearrange("p (h c j) -> p h c j", h=4, c=2)
        S.add("dve", lambda e: e.scalar_tensor_tensor(out=oTs[:, :, s * 8:(s + 1) * 8], in0=a4[:, :, 1, :], scalar=neglam[:, 0:1],
                                                      in1=a4[:, :, 0, :], op0=ALU.mult, op1=ALU.add),
              r=["bgp", "neglam"], w=["oTs"])

    def bg_begin(tile):
        slots = {0: [1, 2, 3], 1: [2, 3], 2: [3], 3: [0, 1, 2, 3], 99: [0, 1, 2, 3]}.get(tile, [])
        bgsched["tile"] = tile
        bgsched["slots"] = slots
        bgsched["k"] = 0
        bgsched["inflight"] = []
        pages = [u for u in bgq if u[0] == "pages"]
        for i in range(min(len(slots), len(pages))):
            u = pages[i]
            slot = slots[bgsched["k"] % len(slots)]
            bgsched["k"] += 1
            bg_gather(u[1], u[2], slot)
            bgsched["inflight"].append((u, slot))

    def bg_emit_one():
        if not bgq or not bgsched["slots"]:
            return False
        u = bgq[0]
        if u[0] == "final":
            bgq.pop(0)
            bg_final(u[1])
            return True
        if not bgsched["inflight"] or bgsched["inflight"][0][0] != u:
            return False
        bgq.pop(0)
        _, slot = bgsched["inflight"].pop(0)
        bg_pages(u[1], u[2], slot)
        gathered = set(x[0] for x in bgsched["inflight"])
        for v in bgq:
            if v[0] == "pages" and v not in gathered:
                bg_gather(v[1], v[2], slot)
                bgsched["inflight"].append((v, slot))
                break
        return True

    def bg_site(tile, cost=1.7):
        if tile != bgsched["tile"] or not bgq:
            return
        bgstate["site"] += cost
        per = BG_US.get(tile, 20.0)
        while bgstate["site"] >= per:
            bgstate["site"] -= per
            if not bg_emit_one():
                break

    def bg_flush():
        bg_begin(99)
        while bgq:
            if not bg_emit_one():
                raise RuntimeError("bg flush stuck")

    def run_tile(samp, t, part="AB"):
        nseg = NS if samp else 1
        L = LS if samp else W
        Wt = nseg * L
        c0 = 0 if samp else 16
        tag = "s" if samp else "p%d" % t

        def seg(ap2d):
            return ap2d.rearrange("p (s l) -> p s l", l=L)

        def mbc(tile3, kc):
            return tile3[:, kc, c0:c0 + nseg].unsqueeze(2).to_broadcast([128, nseg, L])

        if (not samp) and t < 3:
            bg_begin(t)
        if samp and part == "B":
            bg_flush()
            sp_dma(lambda e: e.dma_start(out=halo[:].rearrange("p a s l -> p (a s l)"), in_=sconv_d.rearrange("p a s l -> p (a s l)")), r=["halo%d" % c_ for c_ in range(44)], w=["halo%d" % c_ for c_ in range(44)])
            S.add("dve", lambda e: e.tensor_copy(out=mixin[:, 0:4, 0:WS], in_=mixs[:]), r=["mixs"], w=["mixin%d" % g for g in range(4)])
        elif (not samp) and t == 0:
            S.add("dve", lambda e: e.memset(uT[:, :, 0:16], 0.0), r=["uT"], w=["uT"])
        src = xT_s if samp else xT_p[:, t * W:(t + 1) * W]
        sp_dma(lambda e: e.dma_start(out=xT[:, :, 0:Wt], in_=src.rearrange("(k p) w -> p k w", p=128)), w=["xT%d" % k for k in range(KC)])

        def sumsq_rstd(srcfn, keys, nchunks, dim):
            if MICRO < 1:
                return
            for k in range(nchunks):
                S.add("act", lambda e, k=k: e.activation(out=sqb[k % 2][:, 0:Wt], in_=srcfn(k), func=AF.Square),
                      r=[keys[k]], w=["sqb%d" % (k % 2)])
                S.add("pe", lambda e, k=k: e.matmul(ssp[:, 0:Wt], lhsT=ones_b[:, :], rhs=sqb[k % 2][:, 0:Wt], start=(k == 0), stop=(k == nchunks - 1)),
                      r=["sqb%d" % (k % 2), "ones_b"], w=["ssp"])
            if MICRO < 2:
                return
            S.add("act", lambda e: e.activation(out=rstd[:, 0:Wt], in_=ssp[:, 0:Wt], func=AF.Sqrt, scale=1.0 / dim, bias=epsc[:, 0:1]),
                  r=["ssp", "epsc"], w=["rstd"])
            if MICRO < 3:
                return
            S.add("dve", lambda e: e.reciprocal(out=rstd[:, 0:Wt], in_=rstd[:, 0:Wt]), r=["rstd"], w=["rstd"])

        def pre_norm(Aname, bo):
            sumsq_rstd(lambda k: xT[:, k, 0:Wt], ["xT%d" % k for k in range(KC)], KC, D)
            if MICRO < 4:
                return
            for k in range(KC):
                tm = tmpA[k % 2]
                if not samp:
                    S.add("dve", lambda e, k=k, tm=tm: e.scalar_tensor_tensor(out=tm[:, 0:Wt], in0=xT[:, k, 0:Wt], scalar=MA[Aname][:, k, 16:17],
                                                                             in1=rstd[:, 0:Wt], op0=ALU.mult, op1=ALU.mult),
                          r=["xT%d" % k, "rstd", "M" + Aname], w=["tmpA%d" % (k % 2)])
                    S.add("act", lambda e, k=k, tm=tm: e.activation(out=hT[:, k, 0:Wt], in_=tm[:, 0:Wt], func=AF.Identity, bias=modT[:, bo + k, 16:17], scale=1.0),
                          r=["tmpA%d" % (k % 2), "modT%d" % ((bo + k) // 4)], w=["hT%d" % k])
                    continue
                S.add("dve", lambda e, k=k, tm=tm: e.tensor_tensor(out=tm[:, 0:Wt], in0=xT[:, k, 0:Wt], in1=rstd[:, 0:Wt], op=ALU.mult),
                      r=["xT%d" % k, "rstd"], w=["tmpA%d" % (k % 2)])
                if MICRO < 5:
                    continue
                S.add("dve", lambda e, k=k, tm=tm: e.tensor_tensor(out=seg(tm[:, 0:Wt]), in0=seg(tm[:, 0:Wt]), in1=mbc(MA[Aname], k), op=ALU.mult),
                      r=["tmpA%d" % (k % 2), "M" + Aname], w=["tmpA%d" % (k % 2)])
                S.add("dve", lambda e, k=k, tm=tm: e.tensor_tensor(out=seg(hT[:, k, 0:Wt]), in0=seg(tm[:, 0:Wt]),
                                                                  in1=modT[:, bo + k, c0:c0 + nseg].unsqueeze(2).to_broadcast([128, nseg, L]), op=ALU.add),
                      r=["tmpA%d" % (k % 2), "modT%d" % ((bo + k) // 4)], w=["hT%d" % k])

        def rstd_from_ssp(dim):
            S.add("act", lambda e: e.activation(out=rstd[:, 0:Wt], in_=ssp[:, 0:Wt], func=AF.Sqrt, scale=1.0 / dim, bias=epsc[:, 0:1]),
                  r=["ssp", "epsc"], w=["rstd"])
            S.add("dve", lambda e: e.reciprocal(out=rstd[:, 0:Wt], in_=rstd[:, 0:Wt]), r=["rstd"], w=["rstd"])

        pend = []

        def evac_mix(b, oc, Gname):
            if samp:
                S.add("act", lambda e: e.activation(out=mT[:, oc, 0:Wt], in_=mm[b][:, 0:Wt], func=AF.Copy), r=["mm%d" % b], w=["mT%d" % oc])
                return
            S.add("act", lambda e: e.activation(out=mT[:, oc, 0:Wt], in_=mm[b][:, 0:Wt], func=AF.Copy, scale=MA[Gname][:, oc, 16:17]),
                  r=["mm%d" % b, "M" + Gname], w=["mT%d" % oc])
            S.add("act", lambda e: e.activation(out=sqb[oc % 2][:, 0:Wt], in_=mm[b][:, 0:Wt], func=AF.Square), r=["mm%d" % b], w=["sqb%d" % (oc % 2)])
            pend.append(oc)

        def flush_sq(keep=0):
            while len(pend) > keep:
                oc = pend.pop(0)
                S.add("pe", lambda e, oc=oc: e.matmul(ssp[:, 0:Wt], lhsT=ones_b[:, :], rhs=sqb[oc % 2][:, 0:Wt], start=(oc == 0), stop=(oc == KC - 1)),
                      r=["sqb%d" % (oc % 2), "ones_b"], w=["ssp"])

        def post_norm_res(Gname):
            if not samp:
                flush_sq(0)
                rstd_from_ssp(D)
                for k in range(KC):
                    tm = tmpA[k % 2]
                    S.add("dve", lambda e, k=k, tm=tm: e.tensor_tensor(out=tm[:, 0:Wt], in0=mT[:, k, 0:Wt], in1=rstd[:, 0:Wt], op=ALU.mult),
                          r=["mT%d" % k, "rstd"], w=["tmpA%d" % (k % 2)])
                    S.add("dve", lambda e, k=k, tm=tm: e.tensor_tensor(out=xT[:, k, 0:Wt], in0=xT[:, k, 0:Wt], in1=tm[:, 0:Wt], op=ALU.add),
                          r=["tmpA%d" % (k % 2), "xT%d" % k], w=["xT%d" % k])
                return
            sumsq_rstd(lambda k: mT[:, k, 0:Wt], ["mT%d" % k for k in range(KC)], KC, D)
            for k in range(KC):
                tm = tmpA[k % 2]
                S.add("dve", lambda e, k=k, tm=tm: e.tensor_tensor(out=tm[:, 0:Wt], in0=mT[:, k, 0:Wt], in1=rstd[:, 0:Wt], op=ALU.mult),
                      r=["mT%d" % k, "rstd"], w=["tmpA%d" % (k % 2)])
                S.add("dve", lambda e, k=k, tm=tm: e.tensor_tensor(out=seg(tm[:, 0:Wt]), in0=seg(tm[:, 0:Wt]), in1=mbc(MA[Gname], k), op=ALU.mult),
                      r=["tmpA%d" % (k % 2), "M" + Gname], w=["tmpA%d" % (k % 2)])
                S.add("dve", lambda e, k=k, tm=tm: e.tensor_tensor(out=xT[:, k, 0:Wt], in0=xT[:, k, 0:Wt], in1=tm[:, 0:Wt], op=ALU.add),
                      r=["tmpA%d" % (k % 2), "xT%d" % k], w=["xT%d" % k])

        hkeys = ["hT%d" % k for k in range(KC)]
        front = (part != "B")
        if front:
            pre_norm("A1", 0)
        def load_w(dsrc, bid):
            if bid in pref:
                i = pref.pop(bid)
            else:
                i = nmm[0] % NWB
                nmm[0] += 1
                wflat = wbuf[i][:].rearrange("p k c -> p (k c)")
                if bid not in converted:
                    converted.add(bid)
                    pool_dma(lambda e, i=i: e.dma_start(out=wbuf[i][:], in_=dsrc), w=["wbuf%d" % i])
                    sp_dma(lambda e: e.dma_start(out=wscr[bid], in_=wflat), r=["wbuf%d" % i], w=["wscr%d" % bid])
                else:
                    sp_dma(lambda e: e.dma_start(out=wflat, in_=wscr[bid]), r=["wscr%d" % bid], w=["wbuf%d" % i])
            nb = (bid + 1) % 25
            if (not samp) and nb in converted and nb not in pref and not (t == NT - 1 and nb == 0):
                j = nmm[0] % NWB
                nmm[0] += 1
                wfl2 = wbuf[j][:].rearrange("p k c -> p (k c)")
                sp_dma(lambda e: e.dma_start(out=wfl2, in_=wscr[nb]), r=["wscr%d" % nb], w=["wbuf%d" % j])
                pref[nb] = j
            return i
        mmi = [0]
        def fm_chunk(wtile, wkey, col, evac):
            b = mmi[0] % 2
            mmi[0] += 1
            def f(e):
                last = None
                for kc in range(KC):
                    last = e.matmul(mm[b][:, 0:Wt], lhsT=wtile[:, kc, col:col + 128], rhs=hT[:, kc, 0:Wt], start=(kc == 0), stop=(kc == KC - 1))
                return last
            S.add("pe", f, r=[wkey] + hkeys, w=["mm%d" % b])
            evac(mm[b], "mm%d" % b)
            if not samp:
                bg_site(t)

        if front:
            for blk in [0, 1, 2]:
                wi = load_w(w_in_d[blk], blk)
                for j in range(4):
                    if blk == 0:
                        def ev(bk, bkey, j=j):
                            if samp:
                                S.add("act", lambda e: e.activation(out=uT[:, j, 0:NS * 24].rearrange("p (s l) -> p s l", l=24)[:, :, 16:24],
                                                                    in_=seg(bk[:, 0:Wt]), func=AF.Copy), r=[bkey], w=["uT"])
                            else:
                                S.add("act", lambda e: e.activation(out=uT[:, j, 16:16 + W], in_=bk[:, 0:W], func=AF.Copy), r=[bkey], w=["uT"])
                    elif blk == 1:
                        def ev(bk, bkey, j=j):
                            if samp:
                                for c_ in range(2):
                                    S.add("act", lambda e, c_=c_: e.activation(out=qbd[c_ * 64:(c_ + 1) * 64, j, :, c_ * 8:(c_ + 1) * 8],
                                                                              in_=bk[c_ * 64:(c_ + 1) * 64, 0:Wt].rearrange("p (s l) -> p s l", l=8), func=AF.Copy),
                                          r=[bkey, "qbd"], w=["qbd"])
                            else:
                                S.add("act", lambda e: e.activation(out=qT[:, j, 0:Wt], in_=bk[:, 0:Wt], func=AF.Copy), r=[bkey], w=["qT"])
                    else:
                        def ev(bk, bkey, j=j):
                            si = j % 2
                            S.add("act", lambda e: e.activation(out=stg[si][:, 0:Wt], in_=bk[:, 0:Wt], func=AF.Copy), r=[bkey], w=["stg%d" % si])
                            if samp:
                                S.add("dve", lambda e: e.tensor_copy(out=kTs[:, j, :], in_=bk[:, 0:Wt]), r=[bkey], w=["kTs"])
                                sp_dma(lambda e: e.dma_start(out=kT_so[j * 128:(j + 1) * 128, :], in_=stg[si][:, 0:Wt]), r=["stg%d" % si], out=True)
                            else:
                                S.add("dve", lambda e: e.tensor_copy(out=KVb[:, 4 * t:4 * t + 4, j * 128:(j + 1) * 128], in_=bk[:, 0:W].rearrange("p (b x) -> p b x", x=128)), r=[bkey], w=["kvb%d" % (4 * t + q_) for q_ in range(4)])
                                sp_dma(lambda e: e.dma_start(out=kT_po[j * 128:(j + 1) * 128, t * W:(t + 1) * W], in_=stg[si][:, 0:W]), r=["stg%d" % si], out=True)
                    fm_chunk(wbuf[wi], "wbuf%d" % wi, j * 128, ev)
            wi = load_w(w_in_d[3], 3)
            if samp:
                for s in range(NS):
                    b = mmi[0] % 2
                    mmi[0] += 1
                    def f(e, s=s, b=b, wi=wi):
                        last = None
                        for kc in range(KC):
                            last = e.matmul(mm[b][0:8, :], lhsT=hT[:, kc, s * 8:(s + 1) * 8], rhs=wbuf[wi][:, kc, :], start=(kc == 0), stop=(kc == KC - 1))
                        return last
                    S.add("pe", f, r=["wbuf%d" % wi] + hkeys, w=["mm%d" % b])
                    si = s % 2
                    S.add("act", lambda e, b=b, si=si: e.activation(out=stg[si][0:8, :], in_=mm[b][0:8, :], func=AF.Copy), r=["mm%d" % b], w=["stg%d" % si])
                    S.add("dve", lambda e, b=b, s=s: e.tensor_copy(out=vnb[:, s % 2, :], in_=mm[b][0:8, :]), r=["mm%d" % b], w=["vnb%d" % (s % 2)])
                    sp_dma(lambda e, s=s: e.dma_start(out=vscr[s], in_=vnb[:, s % 2, :]), r=["vnb%d" % (s % 2)], w=["vscr%d" % s])
                    sp_dma(lambda e, s=s, si=si: e.dma_start(out=v_so[s * 8:(s + 1) * 8, :], in_=stg[si][0:8, :]), r=["stg%d" % si], out=True)
            else:
                for bb in range(4):
                    b = mmi[0] % 2
                    mmi[0] += 1
                    def f(e, bb=bb, b=b, wi=wi):
                        last = None
                        for kc in range(KC):
                            last = e.matmul(mm[b][:, :], lhsT=hT[:, kc, bb * 128:(bb + 1) * 128], rhs=wbuf[wi][:, kc, :], start=(kc == 0), stop=(kc == KC - 1))
                        return last
                    S.add("pe", f, r=["wbuf%d" % wi] + hkeys, w=["mm%d" % b])
                    si = bb % 2
                    S.add("act", lambda e, b=b, si=si: e.activation(out=stg[si][:, :], in_=mm[b][:, :], func=AF.Copy), r=["mm%d" % b], w=["stg%d" % si])
                    S.add("dve", lambda e, b=b, bb=bb: e.tensor_copy(out=KVb[:, t * 4 + bb, 512:1024], in_=mm[b][:, :]), r=["mm%d" % b], w=["kvb%d" % (t * 4 + bb)])
                    sp_dma(lambda e, bb=bb, si=si: e.dma_start(out=v_po[t * W + bb * 128:t * W + (bb + 1) * 128, :], in_=stg[si][:, :]), r=["stg%d" % si], out=True)
                    bg_site(t)

            SL = 16 + L
            def useg(g):
                return uT[:, g, 0:nseg * SL].rearrange("p (s l) -> p s l", l=SL)
            def pseg(i):
                return pw[i][:, 0:nseg * SL].rearrange("p (s l) -> p s l", l=SL)
            if samp:
                for g in range(4):
                    sp_dma(lambda e, g=g: e.dma_start(out=uT[:, g, 0:NS * 24].rearrange("p (s l) -> p s l", l=24)[:, :, 1:16], in_=spool_d[:, g, :, :]), r=["uT"], w=["uT"])
            for g in range(4):
                cur = useg(g)
                sh = 1
                for step in range(g + 1):
                    dst = pseg(step % 2)
                    lo = 1 + sh if step == 0 else 1 + 2 * sh - 1
                    lo = {0: 2, 1: 4, 2: 8, 3: 16}[step]
                    S.add("pool", lambda e, cur=cur, dst=dst, lo=lo, sh=sh: e.tensor_tensor(out=dst[:, :, lo:SL], in0=cur[:, :, lo:SL], in1=cur[:, :, lo - sh:SL - sh], op=ALU.add),
                          r=["uT", "pw0", "pw1"], w=["pw%d" % (step % 2)])
                    cur = dst
                    sh *= 2
                wwin = 2 ** (g + 1)
                S.add("dve", lambda e, g=g, cur=cur, wwin=wwin: e.scalar_tensor_tensor(
                    out=seg(dT[:, g, 0:Wt]), in0=cur[:, :, 16:SL], scalar=1.0 / wwin, in1=useg(g)[:, :, 16:SL], op0=ALU.mult, op1=ALU.subtract),
                    r=["pw0", "pw1", "uT"], w=["dT%d" % g])
                if (not samp) and t == 0:
                    S.add("dve", lambda e, g=g, cur=cur: e.tensor_tensor(out=tmpA[0][:, 0:16], in0=cur[:, 0, 16:32], in1=P("rc", 16, g * 16), op=ALU.mult),
                          r=["pw0", "pw1", "params"], w=["tmpA0"])
                    S.add("dve", lambda e, g=g: e.tensor_tensor(out=dT[:, g, 0:16], in0=tmpA[0][:, 0:16], in1=uT[:, g, 16:32], op=ALU.subtract),
                          r=["tmpA0", "uT", "dT%d" % g], w=["dT%d" % g])
                b = mmi[0] % 2
                mmi[0] += 1
                S.add("pe", lambda e, g=g, b=b: e.matmul(mm[b][:, 0:Wt], lhsT=wpool[:, g, :], rhs=dT[:, g, 0:Wt], start=True, stop=True),
                      r=["wpool", "dT%d" % g], w=["mm%d" % b])
                if samp:
                    S.add("act", lambda e, g=g, b=b: e.activation(out=mixs[:, g, :], in_=mm[b][:, 0:Wt], func=AF.Copy, scale=P("pool_scale", 1, g)),
                          r=["mm%d" % b, "params"], w=["mixs"])
                else:
                    S.add("act", lambda e, g=g, b=b: e.activation(out=mixin[:, g, 0:Wt], in_=mm[b][:, 0:Wt], func=AF.Copy, scale=P("pool_scale", 1, g)),
                          r=["mm%d" % b, "params"], w=["mixin%d" % g])
            if samp:
                for g in range(4):
                    sp_dma(lambda e, g=g: e.dma_start(out=pool_so[:, g, :, :], in_=uT[:, g, 0:NS * 24].rearrange("p (s l) -> p s l", l=24)[:, :, 9:24]), r=["uT"], out=True)
            else:
                if t == NT - 1:
                    sp_dma(lambda e: e.dma_start(out=pool_po, in_=uT[:, :, 16 + W - 15:16 + W]), r=["uT"], out=True)
                else:
                    S.add("pool", lambda e: e.tensor_copy(out=uT[:, :, 0:16], in_=uT[:, :, W:W + 16]), r=["uT"], w=["uT"])

        if samp and part == "A":
            build_bg_units()
            return
        def finish_head(h, c, ncols):
            S.add("dve", lambda e: e.reciprocal(out=rl[:, 0:ncols], in_=accL[:, 0:ncols]), r=["accL"], w=["rl"])
            S.add("dve", lambda e: e.tensor_tensor(out=On[c][:, 0:ncols], in0=accO[:, 0:ncols], in1=rl[:, 0:ncols], op=ALU.mult),
                  r=["accO", "rl"], w=["On%d" % c])

        def head_norm(h):
            S.add("dve", lambda e: e.scalar_tensor_tensor(out=oT[:, h, 0:Wt], in0=On[1][:, 0:Wt], scalar=neglam[:, 0:1], in1=On[0][:, 0:Wt],
                                                          op0=ALU.mult, op1=ALU.add), r=["On0", "On1", "neglam"], w=["mT%d" % h])

        if not samp:
            nkb = 4 * t + 4
            blocks = [(h, c, kb) for h in range(4) for c in range(2) for kb in range(nkb)]
            accOs = [(accO, "accO"), (mm[0], "mm0")]
            accLs = [(accL, "accL"), (mm[1], "mm1")]

            def emit_S(i):
                h, c, kb = blocks[i]
                m = kb - 4 * t
                col0 = 128 * m if m >= 1 else 0
                near = (kb >= 4 * t - 1)
                sb_ = i % 2
                pb = i % 3
                S.add("pe", lambda e: e.matmul(stp[sb_][:, col0:W], lhsT=KVb[c * 64:(c + 1) * 64, kb, h * 128:(h + 1) * 128],
                                               rhs=qT[c * 64:(c + 1) * 64, h, col0:W], start=True, stop=True), r=["kvb%d" % kb, "qT"], w=["st%d" % sb_])
                if near:
                    D0 = W * t - 128 * kb
                    jj0 = D0 + col0
                    nbc = min(256 - jj0, W - col0)
                    at_ = atmp[sb_]
                    S.add("dve", lambda e: e.scalar_tensor_tensor(out=at_[:, col0:col0 + nbc], in0=stp[sb_][:, col0:col0 + nbc], scalar=0.125,
                                                                  in1=Mb[:, h, jj0:jj0 + nbc], op0=ALU.mult, op1=ALU.add),
                          r=["st%d" % sb_, "Mb"], w=["stg%d" % sb_])
                    S.add("act", lambda e: e.activation(out=pT[pb][:, col0:col0 + nbc], in_=at_[:, col0:col0 + nbc], func=AF.Exp),
                          r=["stg%d" % sb_], w=["pT%d" % pb])
                    if col0 + nbc < W:
                        S.add("act", lambda e: e.activation(out=pT[pb][:, col0 + nbc:W], in_=stp[sb_][:, col0 + nbc:W], func=AF.Exp, scale=0.125),
                              r=["st%d" % sb_, "pT%d" % pb], w=["pT%d" % pb])
                else:
                    S.add("act", lambda e: e.activation(out=pT[pb][:, col0:W], in_=stp[sb_][:, col0:W], func=AF.Exp, scale=0.125),
                          r=["st%d" % sb_], w=["pT%d" % pb])

            def emit_PV(i):
                h, c, kb = blocks[i]
                m = kb - 4 * t
                col0 = 128 * m if m >= 1 else 0
                pb = i % 3
                g = h * 2 + c
                aO, aOk = accOs[g % 2]
                aL, aLk = accLs[g % 2]
                S.add("pe", lambda e: e.matmul(aO[:, col0:W], lhsT=KVb[:, kb, 512 + h * 128:512 + (h + 1) * 128], rhs=pT[pb][:, col0:W],
                                               start=(kb == 0), stop=(kb == nkb - 1)), r=["kvb%d" % kb, "pT%d" % pb], w=[aOk])
                S.add("pe", lambda e: e.matmul(aL[:, col0:W], lhsT=ones_b[:, :], rhs=pT[pb][:, col0:W],
                                               start=(kb == 0), stop=(kb == nkb - 1)), r=["ones_b", "pT%d" % pb], w=[aLk])
                if kb == nkb - 1:
                    S.add("dve", lambda e: e.reciprocal(out=rl[:, 0:W], in_=aL[:, 0:W]), r=[aLk], w=["rl"])
                    S.add("dve", lambda e: e.tensor_tensor(out=On[c][:, 0:W], in0=aO[:, 0:W], in1=rl[:, 0:W], op=ALU.mult),
                          r=[aOk, "rl"], w=["On%d" % c])
                    if c == 1:
                        head_norm(h)

            LOOK = 2
            for i in range(min(LOOK, len(blocks))):
                emit_S(i)
            for i in range(len(blocks)):
                emit_PV(i)
                if i + LOOK < len(blocks):
                    emit_S(i + LOOK)
                bg_site(t, 0.7)
        oSrc = oTs if samp else oT
        okey = (lambda h: "oTs") if samp else (lambda h: "mT%d" % h)
        for h in range(4):
            sumsq_rstd(lambda k, h=h: oSrc[:, h, 0:Wt], [okey(h)], 1, 128.0)
            S.add("dve", lambda e, h=h: e.tensor_tensor(out=tmpA[0][:, 0:Wt], in0=oSrc[:, h, 0:Wt], in1=rstd[:, 0:Wt], op=ALU.mult),
                  r=[okey(h), "rstd"], w=["tmpA0"])
            S.add("act", lambda e, h=h: e.activation(out=mixin[:, 4 + h, 0:Wt], in_=tmpA[0][:, 0:Wt], func=AF.Copy, scale=ghs[:, 0:1]),
                  r=["tmpA0", "ghs"], w=["mixin%d" % (4 + h)])

        if SUB < 6:
            return
        if (not samp) and t == 0:
            bgsched["slots"] = []
            bgsched["inflight"] = []
        if (not samp) and t == 3:
            bg_begin(3)
        mkeys = ["mixin%d" % k for k in range(KC)]
        for blk in range(2):
            wi = load_w(w_out_d[blk], 4 + blk)
            for j in range(4):
                oc = blk * 4 + j
                b = mmi[0] % 2
                mmi[0] += 1
                def f(e, j=j, b=b, wi=wi):
                    last = None
                    for kc in range(KC):
                        last = e.matmul(mm[b][:, 0:Wt], lhsT=wbuf[wi][:, kc, j * 128:(j + 1) * 128], rhs=mixin[:, kc, 0:Wt], start=(kc == 0), stop=(kc == KC - 1))
                    return last
                S.add("pe", f, r=["wbuf%d" % wi] + mkeys, w=["mm%d" % b])
                flush_sq(0)
                evac_mix(b, oc, "G1")
                if not samp:
                    bg_site(t)
        post_norm_res("G1")

        if SUB < 7:
            return
        pre_norm("A2", 24)
        SL2 = 2 + L
        upi = [0]
        for grp in range(11):
            ui = load_w(w_up_d[grp].rearrange("p a k c -> p (a k c)").rearrange("p (k c) -> p k c", k=KC), 6 + grp)
            wupv = wbuf[ui][:].rearrange("p k c -> p (k c)").rearrange("p (a k c) -> p a k c", a=2, k=KC)
            for pr in range(2):
                i = grp * 2 + pr
                ub = i % 2
                Uv = U[ub][:, 0:2 * nseg * SL2].rearrange("p (a s l) -> p a s l", a=2, l=SL2)
                Yv = yv[ub][:, 0:2 * Wt].rearrange("p (a s l) -> p a s l", a=2, l=L)
                for a in range(2):
                    ch = a * NFC + i
                    b = mmi[0] % 2
                    mmi[0] += 1
                    def f(e, a=a, pr=pr, b=b, wupv=wupv):
                        last = None
                        for kc in range(KC):
                            last = e.matmul(mm[b][:, 0:Wt], lhsT=wupv[:, pr, kc, a * 128:(a + 1) * 128], rhs=hT[:, kc, 0:Wt], start=(kc == 0), stop=(kc == KC - 1))
                        return last
                    S.add("pe", f, r=["wbuf%d" % ui] + hkeys, w=["mm%d" % b])
                    if not samp:
                        bg_site(t)
                    S.add("act", lambda e, ch=ch, a=a, Uv=Uv: e.activation(out=Uv[:, a, :, 0:2], in_=halo[:, ch, 0:nseg, :], func=AF.Copy), r=["halo%d" % ch], w=["U%d_%d" % (ub, a)])
                    S.add("act", lambda e, a=a, b=b, Uv=Uv: e.activation(out=Uv[:, a, :, 2:SL2], in_=seg(mm[b][:, 0:Wt]), func=AF.Copy),
                          r=["mm%d" % b, "U%d_%d" % (ub, a)], w=["U%d_%d" % (ub, a)])
                    S.add("act", lambda e, ch=ch, a=a, Uv=Uv: e.activation(out=halo[:, ch, 0:nseg, :], in_=Uv[:, a, :, L:L + 2], func=AF.Copy),
                          r=["U%d_%d" % (ub, a)], w=["halo%d" % ch])
                    S.add("act", lambda e, ch=ch, a=a, Uv=Uv, Yv=Yv: e.activation(out=Yv[:, a, :, :], in_=Uv[:, a, :, 2:SL2], func=AF.Identity,
                                                                               scale=P("cw2", 1, ch), bias=P("cb", 1, ch)),
                          r=["U%d_%d" % (ub, a), "params"], w=["yv%d_%d" % (ub, a)])
                    S.add("dve", lambda e, ch=ch, a=a, Uv=Uv, Yv=Yv: e.scalar_tensor_tensor(out=Yv[:, a, :, :], in0=Uv[:, a, :, 1:1 + L], scalar=P("cw1", 1, ch),
                                                                                          in1=Yv[:, a, :, :], op0=ALU.mult, op1=ALU.add),
                          r=["U%d_%d" % (ub, a), "yv%d_%d" % (ub, a), "params"], w=["yv%d_%d" % (ub, a)])
                    S.add("dve", lambda e, ch=ch, a=a, Uv=Uv, Yv=Yv: e.scalar_tensor_tensor(out=Yv[:, a, :, :], in0=Uv[:, a, :, 0:L], scalar=P("cw0", 1, ch),
                                                                                          in1=Yv[:, a, :, :], op0=ALU.mult, op1=ALU.add),
                          r=["U%d_%d" % (ub, a), "yv%d_%d" % (ub, a), "params"], w=["yv%d_%d" % (ub, a)])
                S.add("act", lambda e, ub=ub: e.activation(out=yv[ub][:, 0:Wt], in_=yv[ub][:, 0:Wt], func=AF.Gelu_apprx_tanh),
                      r=["yv%d_0" % ub], w=["yv%d_0" % ub])
                S.add("dve", lambda e, ub=ub, i=i: e.tensor_tensor(out=gT[:, i, 0:Wt], in0=yv[ub][:, 0:Wt], in1=yv[ub][:, Wt:2 * Wt], op=ALU.mult),
                      r=["yv%d_0" % ub, "yv%d_1" % ub], w=["gT%d" % i])
        if samp:
            sp_dma(lambda e: e.dma_start(out=conv_so.rearrange("p a s l -> p (a s l)"), in_=halo[:].rearrange("p a s l -> p (a s l)")), r=["halo%d" % c_ for c_ in range(44)], out=True)
        elif t == NT - 1:
            sp_dma(lambda e: e.dma_start(out=conv_po, in_=halo[:, :, 0, :]), r=["halo%d" % c_ for c_ in range(44)], out=True)
        if SUB < 8:
            return
        gkeys = ["gT%d" % i for i in range(NFC)]
        dni = [0]
        for blk in range(8):
            di = load_w(w_down_d[blk], 17 + blk)
            wdv = wbuf[di][:].rearrange("p k c -> p (k c)")
            for j in range(1):
                oc = blk
                b = mmi[0] % 2
                mmi[0] += 1
                def f(e, b=b, wdv=wdv):
                    last = None
                    for i in range(NFC):
                        last = e.matmul(mm[b][:, 0:Wt], lhsT=wdv[:, i * 128:(i + 1) * 128], rhs=gT[:, i, 0:Wt], start=(i == 0), stop=(i == NFC - 1))
                    return last
                S.add("pe", f, r=["wbuf%d" % di] + gkeys, w=["mm%d" % b])
                flush_sq(0)
                evac_mix(b, oc, "G2")
                if not samp:
                    bg_site(t, 4.8)
        if SUB < 9:
            return
        post_norm_res("G2")
        if SUB < 10:
            return
        dst = yT_s if samp else yT_p[:, t * W:(t + 1) * W]
        for k in range(KC):
            sp_dma(lambda e, k=k: e.dma_start(out=dst[k * 128:(k + 1) * 128, :], in_=xT[:, k, 0:Wt]), r=["xT%d" % k], out=True)

    return nc, S, es, run_tile, mod_rest, uT, sconv_d


def emit_program(nc, S, es):
    sems = {}

    def semof(k):
        if k not in sems:
            sems[k] = es.enter_context(nc.semaphore("s_" + "_".join(str(x) for x in k)))
        return sems[k]
    for e in ENG:
        semof(("e", e))
    for q in ("sp", "pool"):
        for i in range(min(NDS, S.dma_i[q])):
            semof(("d", q, i))
    block = es.enter_context(nc.Block())

    def emit(name, eng):
        for waits, fn, tok in S.ops[name]:
            for sk, v in waits:
                eng.wait_ge(semof(sk), v)
            ins = fn(eng)
            sk, v = tok
            ins.then_inc(semof(sk), 16 if sk[0] == "d" else 1)
        if name == "sp":
            for sk, v in S.out_tokens:
                eng.wait_ge(semof(sk), v)

    @block.tensor
    def _(e):
        emit("pe", e)

    @block.scalar
    def _(e):
        emit("act", e)

    @block.vector
    def _(e):
        emit("dve", e)

    @block.gpsimd
    def _(e):
        emit("pool", e)

    @block.sync
    def _(e):
        emit("sp", e)


_CACHE = {}
_DEBUG_CORES = None


def get_program():
    if "nc" not in _CACHE:
        nc, S, es, run_tile, mod_rest, uT, sconv_d = build_program()
        run_tile(True, 0, "A")
        mod_rest()
        for t in range(NT):
            run_tile(False, t)
        run_tile(True, 0, "B")
        emit_program(nc, S, es)
        es.close()
        _CACHE["nc"] = nc
    return _CACHE["nc"]


def _bucket_thresholds():
    d = np.arange(0, 4096)
    dd = np.maximum(d, 1).astype(np.float32)
    large = 16 + (np.log(dd / np.float32(16)) / np.float32(math.log(128 / 16)) * np.float32(16)).astype(np.int32)
    large = np.minimum(large, 31)
    return np.where(d < 16, d, large)


def kernel(x_prompt, x_sample, c_prompt, c_sample, cache_k, cache_v, page_table, state_pool, state_conv,
           w_ada, b_ada, g_pre_mix, g_post_mix, g_pre_ffn, g_post_ffn, w_in, w_out, w_pool, pool_scale,
           lam_q1, lam_k1, lam_q2, lam_k2, g_head, rel_bias, w_up, conv_w, conv_b, w_down):
    f32 = np.float32
    A = lambda a: np.ascontiguousarray(np.asarray(a))
    x_prompt = A(x_prompt); x_sample = A(x_sample)
    def fm(v, n):
        return A(np.asarray(v, f32).reshape(n, 128).T)
    par = np.zeros((128, NPAR), f32)
    par[:, PO["g_pre_mix"]:PO["g_pre_mix"] + 8] = fm(g_pre_mix[0], 8)
    par[:, PO["g_post_mix"]:PO["g_post_mix"] + 8] = fm(g_post_mix[0], 8)
    par[:, PO["g_pre_ffn"]:PO["g_pre_ffn"] + 8] = fm(g_pre_ffn[0], 8)
    par[:, PO["g_post_ffn"]:PO["g_post_ffn"] + 8] = fm(g_post_ffn[0], 8)
    par[:, PO["b_ada"]:PO["b_ada"] + 48] = fm(b_ada[0], 48)
    par[:, PO["pool_scale"]:PO["pool_scale"] + 4] = fm(pool_scale[0], 4)
    par[:, PO["g_head"]:PO["g_head"] + 1] = fm(g_head[0], 1)
    cw = np.asarray(conv_w[0], f32)
    par[:, PO["cw0"]:PO["cw0"] + 44] = fm(cw[0], 44)
    par[:, PO["cw1"]:PO["cw1"] + 44] = fm(cw[1], 44)
    par[:, PO["cw2"]:PO["cw2"] + 44] = fm(cw[2], 44)
    par[:, PO["cb"]:PO["cb"] + 44] = fm(conv_b[0], 44)
    for g in range(4):
        wwin = 2 ** (g + 1)
        par[:, PO["rc"] + g * 16:PO["rc"] + (g + 1) * 16] = (1.0 / np.minimum(np.arange(16) + 1, wwin)).astype(f32)[None, :]
    w_ada_l = A(np.asarray(w_ada[0], f32).reshape(KC, 128, 12, 512).transpose(2, 1, 0, 3))
    w_in_l = A(np.asarray(w_in[0], f32).reshape(KC, 128, 4, 512).transpose(2, 1, 0, 3))
    w_out_l = A(np.asarray(w_out[0], f32).reshape(KC, 128, 2, 512).transpose(2, 1, 0, 3))
    w_pool_l = A(np.asarray(w_pool[0], f32).transpose(1, 0, 2))
    wu = np.asarray(w_up[0], f32).reshape(KC, 128, 2, 11, 2, 128)
    w_up_l = A(wu.transpose(3, 1, 4, 0, 2, 5).reshape(11, 128, 2, KC, 256))
    wd = np.asarray(w_down[0], f32).reshape(NFC, 128, 8, 128)
    w_down_l = np.zeros((8, 128, KC * 512), f32)
    w_down_l[:, :, 0:NFC * 128] = wd.transpose(2, 1, 0, 3).reshape(8, 128, NFC * 128)
    w_down_l = w_down_l.reshape(8, 128, KC, 512)
    lamv = A(np.concatenate([lam_q1[0], lam_k1[0], lam_q2[0], lam_k2[0]]).astype(f32)[None, :])
    bk = _bucket_thresholds()
    oneh = np.zeros((33, FL), f32)
    for i in range(FL):
        d = i - 511
        if d < 0:
            oneh[32, i] = 1.0
        else:
            oneh[bk[d], i] += 1.0
            oneh[31, i] -= 1.0
    jmat = A(np.eye(128, dtype=f32)[::-1])
    kcl = A(np.asarray(cache_k[0], f32).transpose(0, 3, 4, 2, 1).reshape(N_PHYS, 128, 4, 128)).reshape(N_PHYS * 128, 512)
    vcl = np.asarray(cache_v[0], f32).reshape(N_PHYS * 128, 512)
    kvl = np.concatenate([kcl, vcl], axis=1)
    del kcl, vcl
    def core_map(i):
        sq = slice(i * NS, (i + 1) * NS)
        cTi = np.concatenate([np.asarray(c_sample[sq], f32), np.asarray(c_prompt[i:i + 1], f32)], axis=0)
        return {
            "xT_p": A(x_prompt[i].T), "xT_s": A(x_sample[sq].reshape(WS, D).T),
            "cT": A(cTi.reshape(17, KC, 128).transpose(2, 1, 0)), "params": par,
            "w_ada_l": w_ada_l, "w_in_l": w_in_l, "w_out_l": w_out_l, "w_pool_l": w_pool_l,
            "w_up_l": w_up_l, "w_down_l": w_down_l, "lamv": lamv, "rel_bias": A(np.asarray(rel_bias, f32)),
            "onehot": oneh, "jmat": jmat, "kvcache_l": kvl,
            "pt": A(np.asarray(page_table[sq], np.int32).reshape(1, NS * NPG)),
            "spool_l": A(np.asarray(state_pool[0, sq], f32).reshape(NS, 15, 4, 128).transpose(3, 2, 0, 1)),
            "sconv_l": A(np.asarray(state_conv[0, sq], f32).reshape(NS, 2, 44, 128).transpose(3, 2, 0, 1)),
        }
    if _DEBUG_CORES is not None:
        return core_map
    in_maps = [core_map(i) for i in range(NCORES)]
    nc = get_program()
    res = run_bass_kernel_spmd(nc, in_maps, core_ids=list(range(NCORES)))
    R = res.results
    y_p = np.stack([R[i]["yT_p"].T for i in range(NCORES)])
    y_s = np.concatenate([R[i]["yT_s"].T.reshape(NS, LS, D) for i in range(NCORES)])
    k_p = np.stack([R[i]["kT_p"].T.reshape(T, 4, 2, 64) for i in range(NCORES)])[None]
    v_p = np.stack([R[i]["v_p"].reshape(T, 4, 128) for i in range(NCORES)])[None]
    pool_p = np.stack([R[i]["pool_p"].transpose(2, 1, 0).reshape(15, 512) for i in range(NCORES)])[None]
    conv_p = np.stack([R[i]["conv_p"].transpose(2, 1, 0).reshape(2, 2 * DFF) for i in range(NCORES)])[None]
    k_s = np.concatenate([R[i]["kT_s"].T.reshape(NS, LS, 4, 2, 64) for i in range(NCORES)])[None]
    v_s = np.concatenate([R[i]["v_s"].reshape(NS, LS, 4, 128) for i in range(NCORES)])[None]
    pool_s = np.concatenate([R[i]["pool_s"].transpose(2, 3, 1, 0).reshape(NS, 15, 512) for i in range(NCORES)])[None]
    conv_s = np.concatenate([R[i]["conv_s"].transpose(2, 3, 1, 0).reshape(NS, 2, 2 * DFF) for i in range(NCORES)])[None]
    outs = (y_p, y_s, k_p, v_p, pool_p, conv_p, k_s, v_s, pool_s, conv_s)
    return tuple(np.ascontiguousarray(o, dtype=np.float32) for o in outs)
```
